# Optimizing a Trainium2 kernel written in Bass

```python
import math
import jax, jax.numpy as jnp
from jax import lax
import numpy as np

D_MODEL = 1024
BATCH = 4
SEQ = 4096
DEPTH = 1

D_MIX = D_MODEL
ATTN_WIDTH = D_MIX // 2
CONV_WIDTH = D_MIX - ATTN_WIDTH
N_HEADS = 8
HEAD_DIM = ATTN_WIDTH // N_HEADS
CONV_KERNEL = 31
MOBA_BLOCK = 256
MOBA_TOPK = 3
Q_CHUNK = 64
ROPE_THETA = 10000.0
D_FF = -(-8 * D_MODEL // (3 * 256)) * 256
EPS = 1e-6
D_IN = 3 * ATTN_WIDTH + 2 * CONV_WIDTH

kernel_name = "hymba_conformer_moba_swiglu"


def rms_norm(x, g):
    xf = x.astype(jnp.float32)
    y = xf * lax.rsqrt(jnp.mean(xf * xf, axis=-1, keepdims=True) + EPS)
    return (y * g.astype(jnp.float32)).astype(x.dtype)


def layer_norm(x, g, b):
    xf = x.astype(jnp.float32)
    mu = jnp.mean(xf, axis=-1, keepdims=True)
    var = jnp.mean(jnp.square(xf - mu), axis=-1, keepdims=True)
    y = (xf - mu) * lax.rsqrt(var + EPS)
    return (y * g.astype(jnp.float32) + b.astype(jnp.float32)).astype(x.dtype)


def rope_tables(seq_len):
    pos = jnp.arange(seq_len, dtype=jnp.float32)
    inv_freq = ROPE_THETA ** (-jnp.arange(0, HEAD_DIM, 2, dtype=jnp.float32) / HEAD_DIM)
    ang = pos[:, None] * inv_freq[None, :]
    ang = jnp.concatenate([ang, ang], axis=-1)
    return jnp.cos(ang)[:, None, :], jnp.sin(ang)[:, None, :]


def apply_rope(x, cos, sin):
    xf = x.astype(jnp.float32)
    x1, x2 = jnp.split(xf, 2, axis=-1)
    rot = jnp.concatenate([-x2, x1], axis=-1)
    return (xf * cos + rot * sin).astype(x.dtype)


def moba_attention(q, k, v):
    B, H, S, Dh = q.shape
    n_blocks = -(-S // MOBA_BLOCK)
    pad = n_blocks * MOBA_BLOCK - S
    kp = jnp.pad(k, ((0, 0), (0, 0), (0, pad), (0, 0)))
    vp = jnp.pad(v, ((0, 0), (0, 0), (0, pad), (0, 0)))
    kb = kp.reshape(B, H, n_blocks, MOBA_BLOCK, Dh)
    vb = vp.reshape(B, H, n_blocks, MOBA_BLOCK, Dh)
    k_mean = jnp.mean(kb.astype(jnp.float32), axis=3)
    topk = min(MOBA_TOPK, n_blocks)
    scale = HEAD_DIM ** -0.5
    b_idx = jnp.arange(B)[:, None, None, None]
    h_idx = jnp.arange(H)[None, :, None, None]
    key_off = jnp.arange(MOBA_BLOCK)
    block_ids = jnp.arange(n_blocks)
    n_chunks = S // Q_CHUNK

    def chunk_fn(c):
        start = c * Q_CHUNK
        qc = lax.dynamic_slice_in_dim(q, start, Q_CHUNK, axis=2)
        q_pos = start + jnp.arange(Q_CHUNK)
        own = start // MOBA_BLOCK
        gate = jnp.einsum('bhqd,bhnd->bhqn', qc.astype(jnp.float32), k_mean)
        gate = jnp.where((block_ids < own)[None, None, None, :], gate, -jnp.inf)
        _, top_idx = lax.top_k(gate, topk)
        sel_valid = top_idx < own
        k_sel = kb[b_idx, h_idx, top_idx]
        v_sel = vb[b_idx, h_idx, top_idx]
        s_sel = jnp.einsum('bhqd,bhqnkd->bhqnk', qc, k_sel).astype(jnp.float32) * scale
        s_sel = jnp.where(sel_valid[..., None], s_sel, -jnp.inf)
        s_sel = s_sel.reshape(B, H, Q_CHUNK, topk * MOBA_BLOCK)
        k_own = lax.dynamic_slice_in_dim(kp, own * MOBA_BLOCK, MOBA_BLOCK, axis=2)
        v_own = lax.dynamic_slice_in_dim(vp, own * MOBA_BLOCK, MOBA_BLOCK, axis=2)
        s_own = jnp.einsum('bhqd,bhkd->bhqk', qc, k_own).astype(jnp.float32) * scale
        own_pos = own * MOBA_BLOCK + key_off
        s_own = jnp.where((own_pos[None, :] <= q_pos[:, None])[None, None], s_own, -jnp.inf)
        p = jax.nn.softmax(jnp.concatenate([s_own, s_sel], axis=-1), axis=-1)
        p_own = p[..., :MOBA_BLOCK].astype(v.dtype)
        p_sel = p[..., MOBA_BLOCK:].reshape(B, H, Q_CHUNK, topk, MOBA_BLOCK).astype(v.dtype)
        out = (jnp.einsum('bhqk,bhkd->bhqd', p_own, v_own)
               + jnp.einsum('bhqnk,bhqnkd->bhqd', p_sel, v_sel))
        return out

    outs = lax.map(chunk_fn, jnp.arange(n_chunks))
    return outs.transpose(1, 2, 0, 3, 4).reshape(B, H, S, Dh)


def conformer_conv(u, glu_b, dw_w, dw_b, ln_g, ln_b):
    u = u + glu_b.astype(u.dtype)
    a, g = jnp.split(u, 2, axis=-1)
    h = a * jax.nn.sigmoid(g)
    h = jnp.pad(h, ((0, 0), (CONV_KERNEL - 1, 0), (0, 0)))
    h = lax.conv_general_dilated(
        h, dw_w[:, None, :].astype(h.dtype), window_strides=(1,), padding='VALID',
        dimension_numbers=('NWC', 'WIO', 'NWC'), feature_group_count=CONV_WIDTH)
    h = h + dw_b.astype(h.dtype)
    h = layer_norm(h, ln_g, ln_b)
    return jax.nn.silu(h)


def setup_inputs(seed: int = 0) -> dict:
    key = jax.random.key(seed)
    ks = jax.random.split(key, 16)
    f32 = jnp.float32
    L = DEPTH

    def nrm(k, shape, scale):
        return jax.random.normal(k, shape, f32) * scale

    return {
        "x": jax.random.normal(ks[0], (BATCH, SEQ, D_MODEL), f32),
        "norm1_g": 1.0 + nrm(ks[1], (L, D_MODEL), 0.02),
        "w_in": nrm(ks[2], (L, D_MODEL, D_IN), D_MODEL ** -0.5),
        "glu_b": nrm(ks[3], (L, 2 * CONV_WIDTH), 0.02),
        "q_norm_g": 1.0 + nrm(ks[4], (L, HEAD_DIM), 0.02),
        "k_norm_g": 1.0 + nrm(ks[5], (L, HEAD_DIM), 0.02),
        "dw_w": nrm(ks[6], (L, CONV_KERNEL, CONV_WIDTH), CONV_KERNEL ** -0.5),
        "dw_b": nrm(ks[7], (L, CONV_WIDTH), 0.02),
        "conv_ln_g": 1.0 + nrm(ks[8], (L, CONV_WIDTH), 0.02),
        "conv_ln_b": nrm(ks[9], (L, CONV_WIDTH), 0.02),
        "w_out": nrm(ks[10], (L, D_MIX, D_MODEL), D_MIX ** -0.5),
        "norm2_g": 1.0 + nrm(ks[11], (L, D_MODEL), 0.02),
        "w_gate": nrm(ks[12], (L, D_MODEL, D_FF), D_MODEL ** -0.5),
        "w_up": nrm(ks[13], (L, D_MODEL, D_FF), D_MODEL ** -0.5),
        "w_down": nrm(ks[14], (L, D_FF, D_MODEL), D_FF ** -0.5),
    }


def reference(x, norm1_g, w_in, glu_b, q_norm_g, k_norm_g, dw_w, dw_b, conv_ln_g,
              conv_ln_b, w_out, norm2_g, w_gate, w_up, w_down):
    B, S, _ = x.shape
    cos, sin = rope_tables(S)
    for l in range(DEPTH):
        h = rms_norm(x, norm1_g[l])
        proj = h @ w_in[l]
        q, k, v, u = jnp.split(proj, [ATTN_WIDTH, 2 * ATTN_WIDTH, 3 * ATTN_WIDTH], axis=-1)
        q = q.reshape(B, S, N_HEADS, HEAD_DIM)
        k = k.reshape(B, S, N_HEADS, HEAD_DIM)
        v = v.reshape(B, S, N_HEADS, HEAD_DIM)
        q = apply_rope(rms_norm(q, q_norm_g[l]), cos, sin)
        k = apply_rope(rms_norm(k, k_norm_g[l]), cos, sin)
        attn = moba_attention(q.transpose(0, 2, 1, 3), k.transpose(0, 2, 1, 3),
                              v.transpose(0, 2, 1, 3))
        attn = attn.transpose(0, 2, 1, 3).reshape(B, S, ATTN_WIDTH)
        conv = conformer_conv(u, glu_b[l], dw_w[l], dw_b[l], conv_ln_g[l], conv_ln_b[l])
        x = x + jnp.concatenate([attn, conv], axis=-1) @ w_out[l]
        h = rms_norm(x, norm2_g[l])
        x = x + (jax.nn.silu(h @ w_gate[l]) * (h @ w_up[l])) @ w_down[l]
    return x
```

```python
import numpy as np
import ml_dtypes
from contextlib import ExitStack

import concourse.bass as bass
import concourse.mybir as mybir
from concourse.bass_utils import run_bass_kernel_spmd

F32 = mybir.dt.float32
BF16 = mybir.dt.bfloat16
ALU = mybir.AluOpType
ACT = mybir.ActivationFunctionType
AX = mybir.AxisListType

D = 1024
S = 4096
B = 4
T = 2048
W = 4096
DFF = 2816
NFF = DFF // 128
EPS = 1e-6
NEG = -32768.0
BIGF = 1.0e30
DEBUG = False

NDSEM = 8
ATTACH_WAIT = True


class _PEProxy:
    def __init__(self, e):
        self.e = e
        self.first = None

    def matmul(self, *a, **k):
        ins = self.e.matmul(*a, **k)
        if self.first is None:
            self.first = ins
        return ins


class Op:
    __slots__ = ("eng", "fn", "isdma", "deps", "raw", "cost", "xfer", "uid", "pos", "sem", "val", "waits", "bl", "nsucc", "succ")


DEF_COST = {"pe": 0.3, "act": 0.65, "dve": 0.7, "pool": 1.3, "sp": 0.1}
HOP = 0.2


class Prog:
    COMPUTE = ("pe", "act", "dve", "pool")
    ENGS = ("pe", "act", "dve", "pool", "sp")

    def __init__(self, nc):
        self.nc = nc
        self.cur = []
        self.count = {e: 0 for e in self.COMPUTE}
        self.dcount = {"sp": 0, "pool": 0}
        self.last_writer = {}
        self.readers = {}
        self.waited = {}
        self.sems = {}
        self.dma_final = {}
        self.barrier_snap = None
        self.uid = 0

    def alloc_sems(self, stack):
        nc = self.nc
        for e in self.COMPUTE:
            self.sems[e] = stack.enter_context(nc.semaphore("c_" + e))
        for q in ("sp", "pool"):
            for i in range(NDSEM):
                self.sems[(q, i)] = stack.enter_context(nc.semaphore("d_%s%d" % (q, i)))

    def barrier(self):
        assert not self.cur
        snap = []
        for e in self.COMPUTE:
            if self.count[e] > 0:
                snap.append((e, self.count[e]))
        for k, v in self.dma_final.items():
            snap.append((k, v))
        self.barrier_snap = snap
        self.last_writer = {}
        self.readers = {}

    def op(self, eng, fn, reads=(), writes=(), dma=False, cost=None, xfer=3.0):
        o = Op()
        o.eng = eng
        o.fn = fn
        o.isdma = dma
        o.cost = (0.1 if dma else DEF_COST[eng]) if cost is None else cost
        o.xfer = xfer if dma else 0.0
        o.uid = self.uid
        self.uid += 1
        deps = set()
        raw = set()
        for k in reads:
            w = self.last_writer.get(k)
            if w is not None:
                deps.add(w)
                raw.add(w)
        for k in writes:
            w = self.last_writer.get(k)
            if w is not None:
                deps.add(w)
            deps.update(self.readers.get(k, ()))
        deps.discard(o)
        o.deps = deps
        o.raw = raw
        for k in reads:
            self.readers.setdefault(k, []).append(o)
        for k in writes:
            self.last_writer[k] = o
            self.readers[k] = []
        self.cur.append(o)
        return o

    def _schedule(self, ops):
        import heapq
        inphase = set(id(o) for o in ops)
        for o in ops:
            o.deps = [d for d in o.deps if id(d) in inphase]
            o.succ = []
        for o in ops:
            for d in o.deps:
                d.succ.append(o)
        for o in reversed(ops):
            m = 0.0
            for s_ in o.succ:
                if s_.bl > m:
                    m = s_.bl
            o.bl = o.cost + o.xfer + m
        ndep = {id(o): len(o.deps) for o in ops}
        finish = {}
        ready = {e: [] for e in self.ENGS}
        for o in ops:
            if ndep[id(o)] == 0:
                heapq.heappush(ready[o.eng], (-o.bl, o.uid, o))
        free = {e: 0.0 for e in self.ENGS}
        order = {e: [] for e in self.ENGS}
        nleft = len(ops)

        def dready(o):
            t = 0.0
            for d in o.deps:
                f = finish[id(d)] + (HOP if (d.eng != o.eng or d.isdma) else 0.0)
                if f > t:
                    t = f
            return t
        while nleft:
            best = None
            for e in self.ENGS:
                h = ready[e]
                if not h:
                    continue
                cands = heapq.nsmallest(6, h)
                pick = None
                pick_t = None
                for c in cands:
                    t = max(free[e], dready(c[2]))
                    if t <= free[e] + 1e-9:
                        pick, pick_t = c, t
                        break
                    if pick is None or t < pick_t:
                        pick, pick_t = c, t
                if best is None or pick_t < best[0]:
                    best = (pick_t, e, pick)
            t0, e, c = best
            ready[e].remove(c)
            heapq.heapify(ready[e])
            o = c[2]
            free[e] = t0 + o.cost
            finish[id(o)] = t0 + o.cost + o.xfer
            order[e].append(o)
            nleft -= 1
            for s_ in o.succ:
                ndep[id(s_)] -= 1
                if ndep[id(s_)] == 0:
                    heapq.heappush(ready[s_.eng], (-s_.bl, s_.uid, s_))
        return order

    def emit(self, block, final_wait=False):
        ops = self.cur
        self.cur = []
        order = self._schedule(ops)
        sems = self.sems
        for e in self.ENGS:
            for i, o in enumerate(order[e]):
                o.pos = i
                if o.isdma:
                    n = self.dcount[e]
                    self.dcount[e] += 1
                    o.sem = (e, n % NDSEM)
                    o.val = 16 * (n // NDSEM + 1)
                    self.dma_final[o.sem] = o.val
                else:
                    self.count[e] += 1
                    o.sem = e
                    o.val = self.count[e]
        for e in self.ENGS:
            first = True
            for o in order[e]:
                o.waits = []

                def addw(sk, v, o=o, e=e):
                    k = (e, sk)
                    if self.waited.get(k, 0) >= v:
                        return
                    self.waited[k] = v
                    o.waits.append((sk, v))
                if first and self.barrier_snap:
                    for (sk, v) in self.barrier_snap:
                        if sk == e and not o.isdma:
                            continue
                        addw(sk, v)
                first = False
                for d in o.deps:
                    if (not d.isdma) and d.eng == e and not o.isdma:
                        if e == "pe":
                            continue
                        if o.pos - d.pos > 2:
                            if self.waited.get((e, e), 0) >= d.val:
                                continue
                            v = d.val
                            for back in range(3, 8):
                                if o.pos - back < 0:
                                    break
                                q = order[e][o.pos - back]
                                if not q.isdma:
                                    v = max(v, q.val)
                                    break
                            addw(e, v)
                            continue
                    addw(d.sem, d.val)
                if o.isdma and o.val > 16:
                    addw(o.sem, o.val - 16)
                if len(o.waits) > 1:
                    mx = {}
                    for (sk, v) in o.waits:
                        if mx.get(sk, 0) < v:
                            mx[sk] = v
                    o.waits = list(mx.items())
        self.barrier_snap = None
        final = list(self.dma_final.items()) if final_wait else []

        def run(engname, e):
            for o in order[engname]:
                ws = list(o.waits)
                att = None
                if ATTACH_WAIT and ws and not o.isdma:
                    att = ws.pop()
                for (sk, v) in ws:
                    e.wait_ge(sems[sk], v)
                if engname == "pe":
                    px = _PEProxy(e)
                    ins = o.fn(px)
                    first = px.first
                else:
                    ins = o.fn(e)
                    first = ins
                if att is not None:
                    first._wait_ge(sems[att[0]], att[1])
                if o.isdma:
                    ins.then_inc(sems[o.sem], 16)
                else:
                    ins.then_inc(sems[o.sem], 1)
            if engname == "sp":
                for (sk, v) in final:
                    e.wait_ge(sems[sk], v)

        @block.tensor
        def _(e):
            run("pe", e)

        @block.scalar
        def _(e):
            run("act", e)

        @block.vector
        def _(e):
            run("dve", e)

        @block.gpsimd
        def _(e):
            run("pool", e)

        @block.sync
        def _(e):
            run("sp", e)


NPTS = [8, 5]
WIDE_RING = True
XN_ACT = False
NHT = [2, 2]
NSCR = [2, 2]
NCS = [2, 2]
NXT = [4, 4]
V_G1, V_G2, V_GLU, V_GQ, V_GK, V_DWB, V_LNG, V_LNB, V_FLAG, V_E1, V_E64 = 0, 8, 16, 24, 25, 26, 30, 34, 38, 39, 40


def build_program():
    nc = bass.Bass("TRN2", target_bir_lowering=False)

    def din(name, shape, dt=F32):
        return nc.dram_tensor(name, list(shape), dt, kind="ExternalInput").ap()

    xw = din("xw", [W, D])
    w_in = din("w_in", [D, 2560])
    w_out = din("w_out", [D, D])
    w_gu = din("w_gu", [D, NFF * 256])
    w_down = din("w_down", [DFF, D])
    vecs_d = din("vecs", [128, 64])
    dww_d = din("dww", [128, 4 * 31])
    cos_d = din("cos_t", [128, W])
    sin_d = din("sin_t", [128, W])
    elig_d = din("elig", [128, 256])
    inel_d = din("inel", [128, 256])
    cbf_d = din("cbf", [128, 5 * 128], BF16)
    oneh_d = din("oneh", [16, W], BF16)
    y = nc.dram_tensor("y", [T, D], F32, kind="ExternalOutput").ap()
    dbg = None
    if DEBUG:
        dbg = nc.dram_tensor("dbg", [128, 8 * T], BF16, kind="ExternalOutput").ap()

    with ExitStack() as outer:
        P = Prog(nc)
        P.alloc_sems(outer)

        def SB(st, name, shape, dt):
            return st.enter_context(nc.sbuf_tensor(name, list(shape), dt))

        ps = [outer.enter_context(nc.psum_tensor("psb%d" % i, [128, 512], F32)) for i in range(8)]
        psname = {id(p): i for i, p in enumerate(ps)}
        ring_state = {}

        def bank(group, banks):
            i = ring_state.get(group, 0)
            ring_state[group] = i + 1
            return banks[i % len(banks)]

        def pk(p):
            return ("ps", psname[id(p)])

        mixc = [None] * 8
        for c_ in (0, 1):
            mixc[c_] = SB(outer, "mixT%d" % c_, [128, T], BF16)
        vecs = SB(outer, "vecs_sb", [128, 64], F32)
        dww = SB(outer, "dww_sb", [128, 4 * 31], F32)
        cbf = SB(outer, "cbf_sb", [128, 5 * 128], BF16)
        elig = SB(outer, "elig_sb", [128, 256], F32)
        inel = SB(outer, "inel_sb", [128, 256], F32)
        ident = cbf[:, 0:128]
        tri = cbf[:, 128:256]
        blockones = cbf[:, 256:384]
        perm = cbf[:, 384:512]
        allones = cbf[:, 512:640]

        P.op("sp", lambda e: e.dma_start(out=vecs[:], in_=vecs_d), writes=["vecs"], dma=True)
        P.op("sp", lambda e: e.dma_start(out=dww[:], in_=dww_d), writes=["dww"], dma=True)
        P.op("sp", lambda e: e.dma_start(out=cbf[:], in_=cbf_d), writes=["cbf"], dma=True)
        P.op("sp", lambda e: e.dma_start(out=elig[:], in_=elig_d), writes=["elig"], dma=True)
        P.op("sp", lambda e: e.dma_start(out=inel[:], in_=inel_d), writes=["inel"], dma=True)

        def vcol(c):
            return vecs[:, c:c + 1]

        V_HBG = 41
        P.op("dve", lambda e: e.tensor_scalar(out=vecs[:, V_HBG:V_HBG + 4], in0=vecs[:, V_GLU + 4:V_GLU + 8], scalar1=0.5, scalar2=None, op0=ALU.mult),
             reads=["vecs"], writes=["vecs"])
        rstd_all = SB(outer, "rstd_all", [128, 32], F32)
        cneg = SB(outer, "cneg", [128, 2], F32)
        P.op("pool", lambda e: e.memset(cneg[:, 0:1], -1.0), writes=["cneg0"])
        P.op("pool", lambda e: e.memset(cneg[:, 1:2], -0.5), writes=["cneg1"])

        def cbc(k, lo, hi, n):
            return bass.AP(cneg, lo * 2 + k, [[2, hi - lo], [0, n]])

        G3 = [ps[5], ps[6], ps[7]]
        SR = [ps[0], ps[1], ps[2]]
        OR = [ps[3], ps[4]]

        def issue_w(pas, wsb, which=None, after=()):
            wsrc = [(256 * pas, 256), (512 + 256 * pas, 256), (1024 + 256 * pas, 256)]
            if pas == 0:
                wsrc.append((1536, 1024))
            segs = []
            off = 0
            offs = []
            for (c0, n) in wsrc:
                segs.append((off, n))
                offs.append(off)
                off += n
            if which is None:
                which = [1, 2, 0, 3] if pas == 0 else [1, 2, 0]
            for si in which:
                (c0, n) = wsrc[si]
                off = offs[si]
                for hk in range(2):
                    def f(e, c0=c0, n=n, hk=hk, off=off):
                        return e.dma_start(out=wsb[:, 4 * hk:4 * hk + 4, off:off + n],
                                           in_=w_in[512 * hk:512 * hk + 512, c0:c0 + n].rearrange("(c p) n -> p c n", p=128))
                    P.op("pool", f, reads=list(after), writes=[("wsb", pas, off, kc) for kc in range(4 * hk, 4 * hk + 4)], dma=True,
                         xfer=(6.0 if n == 256 else 20.0))
            return segs

        hscope = ExitStack()
        w1scope = ExitStack()
        hglu = hscope.enter_context(nc.sbuf_tensor("hglu", [128, 4, 32 + T], BF16, side="right"))

        for pas in range(2):
            with ExitStack() as sc:
                Kaug = [SB(sc, "Kaug%d_%d" % (pas, h), [128, W], BF16) for h in range(4)]
                Vaug = SB(sc, "Vaug%d" % pas, [128, 32, 384], BF16)
                Qaug = [[SB(sc, "Qaug%d_%d_%d" % (pas, b_, h), [128, 512], BF16) for h in range(4)] for b_ in range(2)]
                if pas == 0:
                    wsb = SB(sc, "wsb0", [128, 8, 1792], BF16)
                else:
                    wsb = wsb1
                xts = [SB(sc, "xt%d_%d" % (pas, i), [128, D], F32) for i in range(NXT[pas])] if pas == 0 else xts1
                xn4 = SB(sc, "xn4_%d" % pas, [128, 4, D], BF16)
                ssq = [SB(sc, "ssq%d_%d" % (pas, i), [128, 1], F32) for i in range(NXT[pas])]
                rstd1 = [SB(sc, "rstd%d_%d" % (pas, i), [128, 1], F32) for i in range(NXT[pas])]
                hTs = [SB(sc, "hT%d_%d" % (pas, i), [128, 8, 512], BF16) for i in range(NHT[pas])]
                cosb = [SB(sc, "cos%d_%d" % (pas, i), [128, 512], F32) for i in range(NCS[pas])] if pas == 0 else cos1
                sinb = [SB(sc, "sin%d_%d" % (pas, i), [128, 512], F32) for i in range(NCS[pas])] if pas == 0 else sin1
                SCR = []
                for i_ in range(NSCR[pas]):
                    SCR.append(dict(
                        sq=SB(sc, "sq%d_%d" % (pas, i_), [128, 512], BF16),
                        rs=SB(sc, "rs%d_%d" % (pas, i_), [128, 512], F32),
                        rc=SB(sc, "rc%d_%d" % (pas, i_), [128, 512], F32),
                        qnb=SB(sc, "qnb%d_%d" % (pas, i_), [128, 512], BF16),
                        t1=SB(sc, "t1_%d_%d" % (pas, i_), [128, 512], F32),
                        t2=SB(sc, "t2_%d_%d" % (pas, i_), [128, 512], F32),
                        ksum=SB(sc, "ksum%d_%d" % (pas, i_), [128, 4], F32)))
                scr_i = [0]
                kmean = [SB(sc, "kmean%d_%d" % (pas, h), [128, 16], BF16) for h in range(4)]
                pT = [SB(sc, "pT%d_%d" % (pas, i), [128, 512], BF16) for i in range(NPTS[pas])]
                NPT = len(pT)
                rd = [SB(sc, "rd%d_%d" % (pas, i), [128, 512], F32) for i in range(2)]
                Gm = SB(sc, "Gm%d" % pas, [128, 64], F32)
                top8 = SB(sc, "top8_%d" % pas, [128, 32], F32)
                Bt = SB(sc, "Bt%d" % pas, [128, 4, 128], BF16)
                sig = SB(sc, "sig%d" % pas, [128, 512], F32) if pas == 0 else None

                segs = issue_w(pas, wsb, which=[1, 2]) if pas == 0 else segs1
                WQ, WK, WV, WU = 0, 256, 512, 768
                hb_ = [0]
                gr_ = ["g", G3]
                seq_ = [0]
                cbk_ = [0]

                def wread(off_, n):
                    return [("wsb", pas, o2, kc) for (o2, nn) in segs if off_ < o2 + nn and off_ + n > o2 for kc in range(8)]

                for q4 in range(4):
                    P.op("pool", lambda e, q4=q4: e.memset(
                        Vaug[:, 8 * q4:8 * q4 + 8, :].rearrange("p k (a b) -> p k a b", a=2)[:, :, :, 64:128], 1.0),
                        writes=[("Vaug_ones", q4)], cost=1.0)
                P.op("pool", lambda e: e.memset(Bt[:, :, :], 0.0), writes=["Bt"])
                for h in range(4):
                    P.op("pool", lambda e, h=h: e.memset(kmean[h][:], 0.0), writes=[("kmean", h)])
                if pas == 0:
                    P.op("pool", lambda e: e.memset(hglu[:, :, 0:32], 0.0), writes=["hglu_halo"])

                def qk_post(pb, is_q, wb, pair):
                    cb = cbk_[0]
                    hA, hB = 2 * pair, 2 * pair + 1
                    gcol = V_GQ if is_q else V_GK
                    si = scr_i[0] % len(SCR)
                    scr_i[0] += 1
                    S_ = SCR[si]
                    sq, rs, rc, qnb, t1, t2, ksum = S_["sq"], S_["rs"], S_["rc"], S_["qnb"], S_["t1"], S_["t2"], S_["ksum"]
                    sd = rs

                    def Y(k):
                        for _ in range(k):
                            yield
                    P.op("act", lambda e: e.activation(out=sq[:], in_=pb[:], func=ACT.Square),
                         reads=[pk(pb)], writes=[("sq", si)], cost=0.6)
                    yield from Y(2)
                    pb2 = bank(gr_[0], gr_[1])
                    P.op("pe", lambda e: e.matmul(pb2[:], lhsT=blockones, rhs=sq[:], start=True, stop=True),
                         reads=[("sq", si), "cbf"], writes=[pk(pb2)], cost=0.25)
                    yield from Y(2)
                    if is_q:
                        P.op("act", lambda e: e.activation(out=sd[:], in_=pb2[:], func=ACT.Ln, scale=1.0, bias=vcol(V_E64)),
                             reads=[pk(pb2), "vecs"], writes=[("rs", si)])
                    else:
                        P.op("act", lambda e: e.activation(out=sd[:], in_=pb2[:], func=ACT.Ln, scale=1.0 / 64.0, bias=vcol(V_E1)),
                             reads=[pk(pb2), "vecs"], writes=[("rs", si)])
                    P.op("act", lambda e: e.activation(out=rs[:], in_=sd[:], func=ACT.Exp, scale=-0.5),
                         reads=[("rs", si)], writes=[("rs", si)])
                    yield from Y(3)
                    P.op("dve", lambda e: e.scalar_tensor_tensor(out=qnb[:], in0=pb[:], scalar=vcol(gcol), in1=rs[:],
                                                                   op0=ALU.mult, op1=ALU.mult),
                         reads=[pk(pb), ("rs", si), "vecs"], writes=[("qnb", si)])
                    P.op("pool", lambda e: e.tensor_tensor(out=rc[:], in0=rs[:], in1=cosb[cb][:], op=ALU.mult),
                         reads=[("rs", si), ("cos", cb)], writes=[("rc", si)], cost=1.3)
                    yield from Y(3)
                    pb3 = bank(gr_[0], gr_[1])
                    P.op("pe", lambda e: e.matmul(pb3[:], lhsT=perm, rhs=qnb[:], start=True, stop=True),
                         reads=[("qnb", si), "cbf"], writes=[pk(pb3)], cost=0.25)
                    yield from Y(2)
                    P.op("dve", lambda e: e.scalar_tensor_tensor(out=t1[:], in0=pb[:], scalar=vcol(gcol), in1=rc[:],
                                                                   op0=ALU.mult, op1=ALU.mult),
                         reads=[pk(pb), ("rc", si), "vecs"], writes=[("t1", si)])
                    P.op("dve", lambda e: e.tensor_tensor(out=t2[:], in0=pb3[:], in1=sinb[cb][:], op=ALU.mult),
                         reads=[pk(pb3), ("sin", cb)], writes=[("t2", si)])
                    yield from Y(3)
                    if is_q:
                        qb = wb % 2
                        P.op("pool", lambda e: e.tensor_tensor(out=Qaug[qb][hA][0:64, :], in0=t1[0:64, :], in1=t2[0:64, :], op=ALU.add),
                             reads=[("t1", si), ("t2", si)], writes=[("Qaug", qb, hA)])
                        P.op("pool", lambda e: e.tensor_tensor(out=Qaug[qb][hB][0:64, :], in0=t1[64:128, :], in1=t2[64:128, :], op=ALU.add),
                             reads=[("t1", si), ("t2", si)], writes=[("Qaug", qb, hB)])
                    else:
                        c0 = wb * 512
                        P.op("pool", lambda e: e.tensor_tensor(out=Kaug[hA][0:64, c0:c0 + 512], in0=t1[0:64, :], in1=t2[0:64, :], op=ALU.add),
                             reads=[("t1", si), ("t2", si)], writes=[("Kaug", hA, wb)])
                        P.op("pool", lambda e: e.tensor_tensor(out=Kaug[hB][0:64, c0:c0 + 512], in0=t1[64:128, :], in1=t2[64:128, :], op=ALU.add),
                             reads=[("t1", si), ("t2", si)], writes=[("Kaug", hB, wb)])
                        P.op("dve", lambda e: e.tensor_reduce(out=ksum[:, 0:2], in_=t1[:].rearrange("p (b k) -> p b k", b=2),
                                                               axis=AX.X, op=ALU.add),
                             reads=[("t1", si)], writes=[("ksum1", si)])
                        P.op("dve", lambda e: e.tensor_reduce(out=ksum[:, 2:4], in_=t2[:].rearrange("p (b k) -> p b k", b=2),
                                                               axis=AX.X, op=ALU.add),
                             reads=[("t2", si)], writes=[("ksum2", si)])
                        yield from Y(3)
                        P.op("dve", lambda e: e.tensor_tensor(out=kmean[hA][0:64, 2 * wb:2 * wb + 2], in0=ksum[0:64, 0:2], in1=ksum[0:64, 2:4], op=ALU.add),
                             reads=[("ksum1", si), ("ksum2", si)], writes=[("kmean", hA)], cost=0.2)
                        P.op("dve", lambda e: e.tensor_tensor(out=kmean[hB][0:64, 2 * wb:2 * wb + 2], in0=ksum[64:128, 0:2], in1=ksum[64:128, 2:4], op=ALU.add),
                             reads=[("ksum1", si), ("ksum2", si)], writes=[("kmean", hB)], cost=0.2)
                    yield

                def proj_fm(off_, ntok, tok0):
                    pb = bank(gr_[0], gr_[1])
                    hbi = hb_[0]
                    hT = hTs[hbi]

                    def f(e):
                        ins = None
                        for kc in range(8):
                            ins = e.matmul(pb[:, 0:ntok], lhsT=wsb[:, kc, off_:off_ + 128],
                                           rhs=hT[:, kc, tok0:tok0 + ntok], start=(kc == 0), stop=(kc == 7))
                        return ins
                    P.op("pe", f, reads=wread(off_, 128) + [("hT", hbi, c) for c in range(8)], writes=[pk(pb)], cost=(1.8 if ntok == 512 else 0.6))
                    return pb

                def gating(qc):
                    qb = qc % 2
                    if qc == 0 and WIDE_RING:
                        gr_[0], gr_[1] = "g8", list(ps)
                    elif qc == 0:
                        gr_[0], gr_[1] = "g", G3
                    for t in range(4):
                        qt = 4 * qc + t
                        gp = bank(gr_[0], gr_[1])

                        def fG(e, gp=gp, t=t):
                            ins = None
                            for h in range(4):
                                ins = e.matmul(gp[:, 16 * h:16 * h + 16], lhsT=Qaug[qb][h][0:64, 128 * t:128 * t + 128],
                                               rhs=kmean[h][0:64, :], start=True, stop=True)
                            return ins
                        P.op("pe", fG, reads=[("Qaug", qb, h) for h in range(4)] + [("kmean", h) for h in range(4)],
                             writes=[pk(gp)], cost=0.25)
                        yield
                        yield
                        for h in range(4):
                            P.op("dve", lambda e, gp=gp, h=h, qt=qt: e.tensor_tensor(
                                out=Gm[:, 16 * h:16 * h + 16], in0=gp[:, 16 * h:16 * h + 16],
                                in1=elig[:, 16 * qt:16 * qt + 16], op=ALU.add),
                                reads=[pk(gp), "elig"], writes=[("Gm", h)], cost=0.25)
                            P.op("dve", lambda e, h=h: e.max(out=top8[:, 8 * h:8 * h + 8], in_=Gm[:, 16 * h:16 * h + 16]),
                                 reads=[("Gm", h)], writes=[("top8", h)], cost=0.3)
                            P.op("dve", lambda e, h=h, qt=qt, t=t: e.scalar_tensor_tensor(
                                out=Bt[:, t, 64 + 16 * h:64 + 16 * h + 16], in0=Gm[:, 16 * h:16 * h + 16],
                                scalar=top8[:, 8 * h + 3:8 * h + 4], in1=inel[:, 16 * qt:16 * qt + 16],
                                op0=ALU.is_lt, op1=ALU.add),
                                reads=[("Gm", h), ("top8", h), "inel", "Bt"], writes=[("Bt", t, h)], cost=0.3)
                        yield
                    for h in range(4):
                        bt = bank(gr_[0], gr_[1])

                        def fT(e, h=h, bt=bt):
                            ins = None
                            for t in range(4):
                                ins = e.matmul(bt[0:80, 128 * t:128 * t + 128], lhsT=Bt[:, t, 16 * h:16 * h + 80],
                                               rhs=ident, start=True, stop=True)
                            return ins
                        P.op("pe", fT, reads=[("Bt", t, hh) for t in range(4) for hh in range(4)] + ["Bt", "cbf"],
                             writes=[pk(bt)], cost=0.45)
                        yield
                        yield
                        P.op("dve", lambda e, h=h, bt=bt: e.tensor_copy(out=Qaug[qb][h][64:80, :], in_=bt[64:80, :]),
                             reads=[pk(bt)], writes=[("Qbias", qb, h)], cost=0.45)
                        yield

                def attention(qc):
                    qb = qc % 2
                    nfull = 16 + 4 * qc
                    for h in range(4):
                        pair = h // 2
                        isB = h % 2
                        po = bank("o", OR)
                        steps = []
                        for kt in range(nfull + 4):
                            i = kt - nfull
                            steps.append((kt, 0 if i < 0 else 128 * i, i))
                        n = len(steps)
                        LAG = 2
                        sbank = {}
                        vlo = pair * 192 + (64 if isB else 0)
                        for j in range(n + LAG):
                            if j < n:
                                kt, q0, i = steps[j]
                                sb_ = bank("s", SR)
                                sbank[j] = sb_

                                def fS(e, kt=kt, q0=q0, i=i, sb_=sb_, h=h):
                                    ins = e.matmul(sb_[:, q0:512], lhsT=Kaug[h][0:80, kt * 128:(kt + 1) * 128],
                                                   rhs=Qaug[qb][h][0:80, q0:512], start=True, stop=(i < 0))
                                    if i >= 0:
                                        ins = e.matmul(sb_[:, q0:q0 + 128], lhsT=ident, rhs=tri, start=False, stop=True)
                                    return ins
                                P.op("pe", fS, reads=[("Kaug", h, kt // 4), ("Kaug_oh", h), ("Qaug", qb, h), ("Qbias", qb, h), "cbf"],
                                     writes=[pk(sb_)], cost=(0.24 if i < 0 else 0.32))
                            if j >= LAG:
                                jj = j - LAG
                                kt, q0, i = steps[jj]
                                sb_ = sbank.pop(jj)
                                pt = pT[jj % NPT]
                                P.op("act", lambda e, sb_=sb_, pt=pt, q0=q0: e.activation(
                                    out=pt[:, q0:512], in_=sb_[:, q0:512], func=ACT.Exp),
                                    reads=[pk(sb_)], writes=[("pT", jj % NPT)], cost=0.5)

                                def fO(e, kt=kt, q0=q0, pt=pt, jj=jj, po=po, vlo=vlo, n=n):
                                    return e.matmul(po[:, q0:512], lhsT=Vaug[:, kt, vlo:vlo + 128], rhs=pt[:, q0:512],
                                                    start=(jj == 0), stop=(jj == n - 1))
                                P.op("pe", fO, reads=[("pT", jj % NPT), ("Vaug", kt), ("Vaug_ones", kt // 8)], writes=[pk(po)], cost=0.23)
                            yield
                        r_ = rd[h % 2]
                        cch = 2 * pas + pair
                        c0 = qc * 512
                        lo, hi = (0, 64) if not isB else (64, 128)
                        dl, dh = (64, 128) if not isB else (0, 64)
                        P.op("dve", lambda e, r_=r_, po=po, dl=dl, dh=dh: e.reciprocal(out=r_[dl:dh, :], in_=po[dl:dh, :]),
                             reads=[pk(po)], writes=[("rd", h % 2)], cost=3.4)
                        P.op("dve", lambda e, r_=r_, po=po, cch=cch, c0=c0, lo=lo, hi=hi, dl=dl, dh=dh: e.tensor_tensor(
                            out=mixc[cch][lo:hi, c0:c0 + 512], in0=po[lo:hi, :], in1=r_[dl:dh, :], op=ALU.mult),
                            reads=[pk(po), ("rd", h % 2)], writes=[("mixT", cch, qc, isB)])

                def prep_gen(wb, do_gating=True):
                    own = wb >= 4
                    sq_ = seq_[0]
                    seq_[0] += 1
                    if wb <= 4 and WIDE_RING:
                        gr_[0], gr_[1] = "g8", list(ps)
                    else:
                        gr_[0], gr_[1] = "g", G3
                    cb = sq_ % len(cosb)
                    cbk_[0] = cb
                    hT = hTs[sq_ % len(hTs)]
                    hb_[0] = sq_ % len(hTs)
                    pre_ = (pas == 1 and wb == 0 and sq_ == 0)
                    if not pre_:
                        P.op("sp", lambda e, wb=wb, cb=cb: e.dma_start(out=cosb[cb][:], in_=cos_d[:, wb * 512:(wb + 1) * 512]),
                             writes=[("cos", cb)], dma=True)
                        P.op("sp", lambda e, wb=wb, cb=cb: e.dma_start(out=sinb[cb][:], in_=sin_d[:, wb * 512:(wb + 1) * 512]),
                             writes=[("sin", cb)], dma=True)
                    for t in range(4):
                        wt = 4 * wb + t
                        xb = wt % len(xts)
                        if not pre_:
                            P.op("sp", lambda e, wt=wt, xb=xb: e.dma_start(out=xts[xb][:], in_=xw[wt * 128:(wt + 1) * 128, :]),
                                 writes=[("xt", xb)], dma=True)
                        if pas == 0:
                            P.op("act", lambda e, xb=xb, t=t: e.activation(out=xn4[:, t, :], in_=xts[xb][:], func=ACT.Square, accum_out=ssq[xb][:]),
                                 reads=[("xt", xb)], writes=[("xn", t), ("ssq", xb)], cost=1.05)
                            P.op("dve", lambda e, xb=xb: e.tensor_scalar(out=rstd1[xb][:], in0=ssq[xb][:], scalar1=1.0 / D, scalar2=EPS, op0=ALU.mult, op1=ALU.add),
                                 reads=[("ssq", xb)], writes=[("rstd", xb)], cost=0.2)
                            P.op("pool", lambda e, xb=xb, wt=wt: e.tensor_tensor(out=rstd_all[:, wt:wt + 1], in0=rstd1[xb][:], in1=cneg[:, 1:2], op=ALU.pow),
                                 reads=[("rstd", xb), "cneg1"], writes=[("rstdall", wt)], cost=0.5)
                        if pas == 1 and wb <= 4 and XN_ACT:
                            P.op("act", lambda e, xb=xb, t=t, wt=wt: e.activation(out=xn4[:, t, :], in_=xts[xb][:], func=ACT.Copy, scale=rstd_all[:, wt:wt + 1]),
                                 reads=[("xt", xb), ("rstdall", wt)], writes=[("xn", t)], cost=1.25)
                        else:
                            P.op("dve", lambda e, xb=xb, t=t, wt=wt: e.tensor_scalar(out=xn4[:, t, :], in0=xts[xb][:], scalar1=rstd_all[:, wt:wt + 1],
                                                                                     scalar2=None, op0=ALU.mult),
                                 reads=[("xt", xb), ("rstdall", wt)], writes=[("xn", t)], cost=0.8)
                        yield
                    for c in range(8):
                        tp = bank(gr_[0], gr_[1])

                        def fTr(e, tp=tp, c=c):
                            ins = None
                            for t in range(4):
                                ins = e.matmul(tp[:, 128 * t:128 * t + 128], lhsT=xn4[:, t, 128 * c:128 * c + 128],
                                               rhs=ident, start=True, stop=True)
                            return ins
                        P.op("pe", fTr, reads=[("xn", t) for t in range(4)] + ["cbf"], writes=[pk(tp)], cost=0.5)
                        if (c % 2 == 0) if wb <= 4 else (c % 4 != 3):
                            P.op("dve", lambda e, tp=tp, c=c: e.tensor_scalar(
                                out=hT[:, c, :], in0=tp[:], scalar1=vcol(V_G1 + c), scalar2=None, op0=ALU.mult),
                                reads=[pk(tp), "vecs"], writes=[("hT", sq_ % len(hTs), c)], cost=0.75)
                        else:
                            P.op("act", lambda e, tp=tp, c=c: e.activation(
                                out=hT[:, c, :], in_=tp[:], func=ACT.Copy, scale=vcol(V_G1 + c)),
                                reads=[pk(tp), "vecs"], writes=[("hT", sq_ % len(hTs), c)])
                        yield
                    for pair in range(2):
                        pb = proj_fm(WK + 128 * pair, 512, 0)
                        yield
                        for _ in qk_post(pb, False, wb, pair):
                            yield
                    for t in range(4):
                        kt = 4 * wb + t
                        pb = bank(gr_[0], gr_[1])

                        def fV(e, pb=pb, t=t):
                            ins = None
                            for kc in range(8):
                                ins = e.matmul(pb[:, 0:256], lhsT=hT[:, kc, 128 * t:128 * t + 128], rhs=wsb[:, kc, WV:WV + 256],
                                               start=(kc == 0), stop=(kc == 7))
                            return ins
                        P.op("pe", fV, reads=wread(WV, 256) + [("hT", sq_ % len(hTs), c) for c in range(8)], writes=[pk(pb)], cost=1.0)
                        vdst = Vaug[:, kt, :].rearrange("p (a b) -> p a b", a=2)
                        vsrc = pb[:, 0:256].rearrange("p (a b) -> p a b", a=2)
                        if wb <= 4:
                            P.op("act", lambda e, vdst=vdst, vsrc=vsrc: e.copy(out=vdst[:, :, 0:64], in_=vsrc[:, :, 0:64]),
                                 reads=[pk(pb), ("Vaug_ones", kt // 8)], writes=[("Vaug", kt)], cost=0.45)
                            P.op("act", lambda e, vdst=vdst, vsrc=vsrc: e.copy(out=vdst[:, :, 128:192], in_=vsrc[:, :, 64:128]),
                                 reads=[pk(pb), ("Vaug_ones", kt // 8)], writes=[("Vaug", kt)], cost=0.45)
                        else:
                            P.op("dve", lambda e, vdst=vdst, vsrc=vsrc: e.tensor_copy(out=vdst[:, :, 0:64], in_=vsrc[:, :, 0:64]),
                                 reads=[pk(pb), ("Vaug_ones", kt // 8)], writes=[("Vaug", kt)], cost=0.35)
                            P.op("dve", lambda e, vdst=vdst, vsrc=vsrc: e.tensor_copy(out=vdst[:, :, 128:192], in_=vsrc[:, :, 64:128]),
                                 reads=[pk(pb), ("Vaug_ones", kt // 8)], writes=[("Vaug", kt)], cost=0.35)
                        yield
                    if own:
                        for pair in range(2):
                            pb = proj_fm(WQ + 128 * pair, 512, 0)
                            yield
                            for _ in qk_post(pb, True, wb, pair):
                                yield
                        for _ in range(4):
                            yield
                        if do_gating:
                            for _ in gating(wb - 4):
                                yield
                    if pas == 0 and (own or wb == 3):
                        for c in range(4):
                            if own:
                                ntok, tok0 = 512, 0
                            else:
                                ntok, tok0 = 128, 384
                            pa = proj_fm(WU + 128 * c, ntok, tok0)
                            pg = proj_fm(WU + 512 + 128 * c, ntok, tok0)
                            P.op("act", lambda e, pg=pg, c=c, ntok=ntok: e.activation(
                                out=sig[:, 0:ntok], in_=pg[:, 0:ntok], func=ACT.Tanh, bias=vcol(V_HBG + c), scale=0.5),
                                reads=[pk(pg), "vecs"], writes=["sig"], cost=0.7)
                            P.op("dve", lambda e, pa=pa, c=c, ntok=ntok: e.scalar_tensor_tensor(
                                out=sig[:, 0:ntok], in0=pa[:, 0:ntok], scalar=vcol(V_GLU + c), in1=sig[:, 0:ntok],
                                op0=ALU.add, op1=ALU.mult),
                                reads=[pk(pa), "sig", "vecs"], writes=["sig"], cost=0.75)
                            if own:
                                dst0 = 32 + (wb - 4) * 512
                                P.op("dve", lambda e, pa=pa, c=c, dst0=dst0: e.scalar_tensor_tensor(
                                    out=hglu[:, c, dst0:dst0 + 512], in0=pa[:, 0:512], scalar=vcol(V_GLU + c), in1=sig[:, 0:512],
                                    op0=ALU.add, op1=ALU.add),
                                    reads=[pk(pa), "sig", "vecs"], writes=[("hglu", c)], cost=0.75)
                            else:
                                P.op("dve", lambda e, pa=pa, c=c: e.scalar_tensor_tensor(
                                    out=hglu[:, c, 0:32], in0=pa[:, 96:128], scalar=vcol(V_GLU + c), in1=sig[:, 96:128],
                                    op0=ALU.add, op1=ALU.add),
                                    reads=[pk(pa), "sig", "vecs", "hglu_halo"], writes=[("hglu", c)], cost=0.3)
                                P.op("dve", lambda e, c=c: e.tensor_scalar(
                                    out=hglu[:, c, 0:32], in0=hglu[:, c, 0:32], scalar1=vcol(V_FLAG), scalar2=None, op0=ALU.mult),
                                    reads=[("hglu", c), "vecs"], writes=[("hglu", c)], cost=0.3)
                            yield

                def att_gen(qc):
                    for _ in attention(qc):
                        yield

                def run_gens(gens):
                    gens = list(gens)
                    while gens:
                        for g in list(gens):
                            try:
                                next(g)
                            except StopIteration:
                                gens.remove(g)

                run_gens([prep_gen(0)])
                for h in range(4):
                    P.op("sp", lambda e, h=h: e.dma_start(out=Kaug[h][64:80, :], in_=oneh_d),
                         reads=[("Kaug", 0, 0)], writes=[("Kaug_oh", h)], dma=True)
                if pas == 0:
                    issue_w(0, wsb, which=[0, 3], after=[("Kaug", 0, 0)])
                run_gens([prep_gen(1)])
                run_gens([prep_gen(4, do_gating=False)])
                for wb in range(2, 4):
                    run_gens([prep_gen(wb)])
                if pas == 1:
                    for half in range(2):
                        P.op("pool", lambda e, half=half: e.dma_start(
                            out=woutb[:, 4 * half:4 * half + 4, :],
                            in_=w_out[512 * half:512 * half + 512, :].rearrange("(c p) n -> p c n", p=128)),
                            reads=[("Kaug", 0, 3)], writes=[("woutb", half)], dma=True, xfer=15.0)
                run_gens([gating(0)])
                run_gens([att_gen(0), prep_gen(5)])
                run_gens([att_gen(1), prep_gen(6)])
                run_gens([att_gen(2), prep_gen(7)])
                run_gens([att_gen(3)])

                with nc.Block() as blk:
                    P.emit(blk)
                P.barrier()

            if pas == 0:
                for c_ in (2, 3, 4, 5, 6, 7):
                    mixc[c_] = SB(outer, "mixT%d" % c_, [128, T], BF16)
                wsb1 = SB(w1scope, "wsb1", [128, 8, 768], BF16)
                xts1 = [SB(w1scope, "xt1_%d" % i, [128, D], F32) for i in range(NXT[1])]
                cos1 = [SB(w1scope, "cos1_%d" % i, [128, 512], F32) for i in range(NCS[1])]
                sin1 = [SB(w1scope, "sin1_%d" % i, [128, 512], F32) for i in range(NCS[1])]
                with ExitStack() as sc:
                    diagW = SB(sc, "diagW", [128, 4 * 31, 128], BF16)
                    y32 = [SB(sc, "y32_%d" % i, [128, 4, 512], F32) for i in range(2)]
                    ybf = [SB(sc, "ybf%d" % i, [128, 512], BF16) for i in range(2)]
                    ysq = [SB(sc, "ysq%d" % i, [128, 512], BF16) for i in range(2)]
                    mean = SB(sc, "mean", [128, 512], F32)
                    msq = SB(sc, "msq", [128, 512], F32)
                    var = SB(sc, "var", [128, 512], F32)
                    rstdc = SB(sc, "rstdc", [128, 512], F32)
                    zc = [SB(sc, "zc%d" % i, [128, 512], F32) for i in range(2)]
                    CR = [ps[0], ps[1], ps[2], ps[3]]
                    segs1 = issue_w(1, wsb1)
                    for t_ in range(4):
                        P.op("sp", lambda e, t_=t_: e.dma_start(out=xts1[t_ % len(xts1)][:], in_=xw[t_ * 128:(t_ + 1) * 128, :]),
                             writes=[("xt1pre", t_)], dma=True)
                    P.op("sp", lambda e: e.dma_start(out=cos1[0][:], in_=cos_d[:, 0:512]), writes=[("cos1pre",)], dma=True)
                    P.op("sp", lambda e: e.dma_start(out=sin1[0][:], in_=sin_d[:, 0:512]), writes=[("sin1pre",)], dma=True)
                    P.op("dve", lambda e: e.tensor_scalar(out=dww[:], in0=dww[:], scalar1=0.5, scalar2=None, op0=ALU.mult),
                         reads=["dww"], writes=["dww"], cost=0.3)
                    for c in range(4):
                        for i in range(31):
                            if i % 2 == 0:
                                P.op("dve", lambda e, c=c, i=i: e.tensor_scalar(
                                    out=diagW[:, c * 31 + i, :], in0=ident, scalar1=dww[:, c * 31 + i:c * 31 + i + 1],
                                    scalar2=None, op0=ALU.mult),
                                    reads=["cbf", "dww"], writes=[("diagW", c, i)], cost=0.3)
                            else:
                                P.op("act", lambda e, c=c, i=i: e.activation(
                                    out=diagW[:, c * 31 + i, :], in_=ident, func=ACT.Copy, scale=dww[:, c * 31 + i:c * 31 + i + 1]),
                                    reads=["cbf", "dww"], writes=[("diagW", c, i)], cost=0.4)
                    pending = []

                    def ln_block(tb, s1, s2):
                        P.op("act", lambda e, s1=s1: e.activation(out=mean[:], in_=s1[:], func=ACT.Copy, scale=1.0 / 512.0),
                             reads=[pk(s1)], writes=["mean"])
                        P.op("dve", lambda e: e.tensor_tensor(out=msq[:], in0=mean[:], in1=mean[:], op=ALU.mult),
                             reads=["mean"], writes=["msq"], cost=0.7)
                        P.op("dve", lambda e, s2=s2: e.scalar_tensor_tensor(out=var[:], in0=s2[:], scalar=1.0 / 512.0, in1=msq[:],
                                                                             op0=ALU.mult, op1=ALU.subtract),
                             reads=[pk(s2), "msq"], writes=["var"])
                        P.op("act", lambda e: e.activation(out=var[:], in_=var[:], func=ACT.Ln, scale=1.0, bias=vcol(V_E1)),
                             reads=["var", "vecs"], writes=["var"])
                        P.op("act", lambda e: e.activation(out=rstdc[:], in_=var[:], func=ACT.Exp, scale=-0.5),
                             reads=["var"], writes=["rstdc"])
                        for c in range(4):
                            z = zc[c % 2]
                            P.op("pool" if c % 2 == 0 else "dve", lambda e, c=c, z=z, tb=tb: e.tensor_tensor(out=z[:], in0=y32[tb % 2][:, c, :], in1=mean[:], op=ALU.subtract),
                                 reads=[("y32", tb % 2, c), "mean"], writes=[("zc", c % 2)], cost=(1.3 if c % 2 == 0 else 0.7))
                            P.op("dve", lambda e, c=c, z=z: e.tensor_tensor(out=z[:], in0=z[:], in1=rstdc[:], op=ALU.mult),
                                 reads=[("zc", c % 2), "rstdc"], writes=[("zc", c % 2)])
                            P.op("act", lambda e, c=c, z=z, tb=tb: e.activation(
                                out=mixc[4 + c][:, tb * 512:(tb + 1) * 512], in_=z[:], func=ACT.Silu,
                                scale=vcol(V_LNG + c), bias=vcol(V_LNB + c)),
                                reads=[("zc", c % 2), "vecs"], writes=[("mixT", 4 + c, tb)])

                    for tb in range(4):
                        s1 = bank("c2", [ps[4], ps[6]])
                        s2 = bank("c3", [ps[5], ps[7]])
                        for c in range(4):
                            pc = bank("c", CR)

                            def fC(e, pc=pc, c=c, tb=tb):
                                ins = None
                                for i in range(31):
                                    a0 = 32 + tb * 512 - 30 + i
                                    ins = e.matmul(pc[:], lhsT=diagW[:, c * 31 + i, :], rhs=hglu[:, c, a0:a0 + 512],
                                                   start=(i == 0), stop=(i == 30))
                                return ins
                            P.op("pe", fC, reads=[("diagW", c, i) for i in range(31)] + [("hglu", c), "hglu_halo"], writes=[pk(pc)], cost=6.8)
                            for fn in pending:
                                fn()
                            pending = []
                            P.op("dve", lambda e, pc=pc, c=c, tb=tb: e.tensor_scalar(
                                out=y32[tb % 2][:, c, :], in0=pc[:], scalar1=vcol(V_DWB + c), scalar2=None, op0=ALU.add),
                                reads=[pk(pc), "vecs"], writes=[("y32", tb % 2, c)])
                            P.op("act", lambda e, c=c, tb=tb: e.copy(out=ybf[c % 2][:], in_=y32[tb % 2][:, c, :]),
                                 reads=[("y32", tb % 2, c)], writes=[("ybf", c % 2)], cost=0.6)
                            P.op("dve", lambda e, c=c, tb=tb: e.tensor_tensor(out=ysq[c % 2][:], in0=y32[tb % 2][:, c, :], in1=y32[tb % 2][:, c, :], op=ALU.mult),
                                 reads=[("y32", tb % 2, c)], writes=[("ysq", c % 2)])

                            def stats(c=c, s1=s1, s2=s2, tb=tb):
                                P.op("pe", lambda e: e.matmul(s1[:], lhsT=allones, rhs=ybf[c % 2][:], start=(c == 0), stop=(c == 3)),
                                     reads=[("ybf", c % 2), "cbf"], writes=[pk(s1)])
                                P.op("pe", lambda e: e.matmul(s2[:], lhsT=allones, rhs=ysq[c % 2][:], start=(c == 0), stop=(c == 3)),
                                     reads=[("ysq", c % 2), "cbf"], writes=[pk(s2)])
                                if c == 3:
                                    ln_block(tb, s1, s2)
                            pending.append(stats)
                    for fn in pending:
                        fn()
                    with nc.Block() as blk:
                        P.emit(blk)
                    P.barrier()
                hscope.close()
                woutb = outer.enter_context(nc.sbuf_tensor("woutb", [128, 8, D], BF16, side="right"))

        w1scope.close()
        if DEBUG:
            for c in range(8):
                P.op("sp", lambda e, c=c: e.dma_start(out=dbg[:, c * T:(c + 1) * T], in_=mixc[c][:, :]), dma=True)
        with ExitStack() as sc:
            wdb = SB(sc, "wdb", [128, NFF, D], BF16)
            wgu = [SB(sc, "wgu%d" % i, [128, 8, 256], BF16) for i in range(4)]
            actT = SB(sc, "actT", [128, NFF, 512], BF16)
            x1 = [SB(sc, "x1_%d" % i, [128, 4, D], F32) for i in range(2)]
            xt2 = [SB(sc, "xt2_%d" % i, [128, D], F32) for i in range(2)]
            xn2 = [SB(sc, "xn2_%d" % i, [128, D], BF16) for i in range(2)]
            ssq2 = [SB(sc, "ssq2_%d" % i, [128, 1], F32) for i in range(2)]
            rstd2 = [SB(sc, "rstd2_%d" % i, [128, 1], F32) for i in range(2)]
            sg = [SB(sc, "sg%d" % i, [128, 512], F32) for i in range(2)]
            osb = [SB(sc, "osb%d" % i, [128, 512], F32) for i in range(2)]
            A8 = list(ps)


            def issue_wd():
                for q in range(2):
                    P.op("pool", lambda e, q=q: e.dma_start(
                        out=wdb[:, 11 * q:11 * q + 11, :],
                        in_=w_down[1408 * q:1408 * q + 1408, :].rearrange("(c p) n -> p c n", p=128)),
                        writes=[("wdb", q)], dma=True, xfer=40.0)

            fcount = [0]

            def outnorm_gen(tb):
                xb1 = x1[tb % 2]
                for t in range(4):
                    tt = 4 * tb + t
                    xb = tt % 2
                    P.op("sp", lambda e, tt=tt, xb=xb: e.dma_start(out=xt2[xb][:], in_=xw[T + tt * 128:T + (tt + 1) * 128, :]),
                         writes=[("xt2", xb)], dma=True)
                    for half in range(2):
                        po = bank("a", A8)

                        def fOP(e, po=po, tt=tt, half=half):
                            ins = None
                            for c in range(8):
                                ins = e.matmul(po[:], lhsT=mixc[c][:, tt * 128:(tt + 1) * 128],
                                               rhs=woutb[:, c, 512 * half:512 * half + 512], start=(c == 0), stop=(c == 7))
                            return ins
                        P.op("pe", fOP, reads=[("mixT_tile", tt), ("woutb", 0), ("woutb", 1)], writes=[pk(po)], cost=1.8)
                        yield
                        P.op("dve", lambda e, po=po, xb1=xb1, t=t, half=half, xb=xb: e.tensor_tensor(
                            out=xb1[:, t, 512 * half:512 * half + 512], in0=po[:], in1=xt2[xb][:, 512 * half:512 * half + 512], op=ALU.add),
                            reads=[pk(po), ("xt2", xb)], writes=[("x1", tb % 2, t)])
                    yield
                    P.op("act", lambda e, xb=xb, xb1=xb1, t=t: e.activation(out=xn2[xb][:], in_=xb1[:, t, :], func=ACT.Square, accum_out=ssq2[xb][:]),
                         reads=[("x1", tb % 2, t)], writes=[("xn2", xb), ("ssq2", xb)], cost=1.05)
                    P.op("dve", lambda e, xb=xb: e.tensor_scalar(out=rstd2[xb][:], in0=ssq2[xb][:], scalar1=1.0 / D, scalar2=EPS, op0=ALU.mult, op1=ALU.add),
                         reads=[("ssq2", xb)], writes=[("rstd2", xb)], cost=0.2)
                    P.op("pool", lambda e, xb=xb: e.tensor_tensor(out=rstd2[xb][:], in0=rstd2[xb][:], in1=cneg[:, 1:2], op=ALU.pow),
                         reads=[("rstd2", xb), "cneg1"], writes=[("rstd2", xb)], cost=0.5)
                    yield
                    yield
                    P.op("dve", lambda e, xb=xb, xb1=xb1, t=t: e.tensor_scalar(out=xn2[xb][:], in0=xb1[:, t, :], scalar1=rstd2[xb][:],
                                                                                scalar2=None, op0=ALU.mult),
                         reads=[("x1", tb % 2, t), ("rstd2", xb)], writes=[("xn2", xb)])
                    yield
                    yield
                    for half in range(2):
                        tp = bank("a", A8)

                        def fTr(e, tp=tp, half=half, xb=xb):
                            ins = None
                            for cc in range(4):
                                c = 4 * half + cc
                                ins = e.matmul(tp[:, 128 * cc:128 * cc + 128], lhsT=xn2[xb][:, 128 * c:128 * c + 128],
                                               rhs=ident, start=True, stop=True)
                            return ins
                        P.op("pe", fTr, reads=[("xn2", xb), "cbf"], writes=[pk(tp)], cost=0.5)
                        yield
                        for cc in range(4):
                            c = 4 * half + cc
                            if cc % 2 == 0:
                                P.op("dve", lambda e, tp=tp, cc=cc, c=c, tt=tt: e.tensor_scalar(
                                    out=mixc[c][:, tt * 128:(tt + 1) * 128], in0=tp[:, 128 * cc:128 * cc + 128],
                                    scalar1=vcol(V_G2 + c), scalar2=None, op0=ALU.mult),
                                    reads=[pk(tp), "vecs"], writes=[("mixT_tile", tt)])
                            else:
                                P.op("act", lambda e, tp=tp, cc=cc, c=c, tt=tt: e.activation(
                                    out=mixc[c][:, tt * 128:(tt + 1) * 128], in_=tp[:, 128 * cc:128 * cc + 128],
                                    func=ACT.Copy, scale=vcol(V_G2 + c)),
                                    reads=[pk(tp), "vecs"], writes=[("mixT_tile", tt)])
                    yield

            def gateup_gen(tb):
                for f_ in range(NFF):
                    slot = fcount[0] % 4
                    fcount[0] += 1
                    wt_ = wgu[slot]
                    P.op("pool", lambda e, f_=f_, wt_=wt_: e.dma_start(
                        out=wt_[:, :, :], in_=w_gu[:, f_ * 256:(f_ + 1) * 256].rearrange("(c p) n -> p c n", p=128)),
                        writes=[("wgu", slot)], dma=True, xfer=6.0)
                    pg = bank("a", A8)
                    pu = bank("a", A8)

                    def fGU(e, pg=pg, pu=pu, wt_=wt_, tb=tb):
                        ins = None
                        for c in range(8):
                            ins = e.matmul(pg[:], lhsT=wt_[:, c, 0:128], rhs=mixc[c][:, tb * 512:(tb + 1) * 512],
                                           start=(c == 0), stop=(c == 7))
                        for c in range(8):
                            ins = e.matmul(pu[:], lhsT=wt_[:, c, 128:256], rhs=mixc[c][:, tb * 512:(tb + 1) * 512],
                                           start=(c == 0), stop=(c == 7))
                        return ins
                    tiles = [("mixT_tile", 4 * tb + t) for t in range(4)]
                    P.op("pe", fGU, reads=[("wgu", slot)] + tiles, writes=[pk(pg), pk(pu)], cost=3.6)
                    sgi = sg[f_ % 2]
                    P.op("act", lambda e, pg=pg, sgi=sgi: e.activation(out=sgi[:], in_=pg[:], func=ACT.Silu),
                         reads=[pk(pg)], writes=[("sg", f_ % 2)])
                    P.op("dve", lambda e, pu=pu, sgi=sgi, f_=f_: e.tensor_tensor(out=actT[:, f_, :], in0=pu[:], in1=sgi[:], op=ALU.mult),
                         reads=[pk(pu), ("sg", f_ % 2)], writes=[("actT", f_)])
                    yield

            def down_gen(tb):
                xb1 = x1[tb % 2]
                for t in range(4):
                    tt = 4 * tb + t
                    for half in range(2):
                        pd = bank("a", A8)

                        def fD(e, pd=pd, t=t, half=half):
                            ins = None
                            for f_ in range(NFF):
                                ins = e.matmul(pd[:], lhsT=actT[:, f_, 128 * t:128 * t + 128],
                                               rhs=wdb[:, f_, 512 * half:512 * half + 512], start=(f_ == 0), stop=(f_ == NFF - 1))
                            return ins
                        P.op("pe", fD, reads=[("actT", f_) for f_ in range(NFF)] + [("wdb", 0), ("wdb", 1)], writes=[pk(pd)], cost=4.9)
                        ob = osb[(2 * t + half) % 2]
                        P.op("dve", lambda e, pd=pd, ob=ob, xb1=xb1, t=t, half=half: e.tensor_tensor(
                            out=ob[:], in0=pd[:], in1=xb1[:, t, 512 * half:512 * half + 512], op=ALU.add),
                            reads=[pk(pd), ("x1", tb % 2, t)], writes=[("osb", (2 * t + half) % 2)])
                        P.op("sp", lambda e, ob=ob, tt=tt, half=half: e.dma_start(
                            out=y[tt * 128:(tt + 1) * 128, 512 * half:512 * half + 512], in_=ob[:]),
                            reads=[("osb", (2 * t + half) % 2)], dma=True)
                        yield

            def run_gens2(gens):
                gens = list(gens)
                while gens:
                    for g in list(gens):
                        try:
                            next(g)
                        except StopIteration:
                            gens.remove(g)

            run_gens2([outnorm_gen(0)])
            issue_wd()
            for tb in range(4):
                gl = [gateup_gen(tb)]
                if tb + 1 < 4:
                    gl.append(outnorm_gen(tb + 1))
                run_gens2(gl)
                run_gens2([down_gen(tb)])
            with nc.Block() as blk:
                P.emit(blk, final_wait=True)
    return nc


_NC_CACHE = {}


def _consts():
    bf = ml_dtypes.bfloat16
    p = np.arange(128)
    ident = np.eye(128, dtype=np.float32)
    tri = np.where(p[:, None] <= p[None, :], 0.0, NEG).astype(np.float32)
    blockones = (p[:, None] // 64 == p[None, :] // 64).astype(np.float32)
    partner = (p // 64) * 64 + (p % 64 + 32) % 64
    perm = (p[:, None] == partner[None, :]).astype(np.float32)
    allones = np.ones((128, 128), np.float32)
    cbf = np.concatenate([ident, tri, blockones, perm, allones], axis=1).astype(bf)
    keyblk = np.arange(W) // 256
    oneh = np.where(keyblk[None, :] == np.arange(16)[:, None], NEG, 0.0).astype(np.float32).astype(bf)
    return cbf, oneh


def _rope_tables(half):
    pos = (np.arange(W, dtype=np.int64) + half * T - T)
    posf = np.maximum(pos, 0).astype(np.float32)
    inv_freq = (np.float32(10000.0) ** (-np.arange(0, 64, 2, dtype=np.float32) / np.float32(64))).astype(np.float32)
    ang = (posf[:, None] * inv_freq[None, :]).astype(np.float32)
    ang = np.concatenate([ang, ang], axis=-1)
    cos = np.cos(ang).astype(np.float32).T
    sin = np.sin(ang).astype(np.float32).T
    sgn = np.where(np.arange(64) < 32, -1.0, 1.0).astype(np.float32)[:, None]
    sin = sin * sgn
    cos2 = np.concatenate([cos, cos], axis=0)
    sin2 = np.concatenate([sin, sin], axis=0)
    return np.ascontiguousarray(cos2), np.ascontiguousarray(sin2)


def _elig(half):
    el = np.zeros((16, 16), np.float32)
    for qt in range(16):
        ob = qt // 2
        for j in range(16):
            if j < 8:
                v = 0.0 if half == 1 else -BIGF
            elif j < 8 + ob:
                v = 0.0
            elif j == 8 + ob:
                v = BIGF
            else:
                v = -BIGF
            el[qt, j] = v
    inel = (el < -1e29).astype(np.float32)
    el_b = np.broadcast_to(el.reshape(1, 256), (128, 256)).copy()
    in_b = np.broadcast_to(inel.reshape(1, 256), (128, 256)).copy()
    return el_b, in_b


def kernel(x, norm1_g, w_in, glu_b, q_norm_g, k_norm_g, dw_w, dw_b, conv_ln_g, conv_ln_b,
           w_out, norm2_g, w_gate, w_up, w_down):
    f32 = np.float32
    x = np.asarray(x, f32)

    def fm(v, n):
        return np.asarray(v, f32).reshape(n, 128).T

    vecs = np.zeros((128, 64), f32)
    vecs[:, V_G1:V_G1 + 8] = fm(norm1_g[0], 8)
    vecs[:, V_G2:V_G2 + 8] = fm(norm2_g[0], 8)
    vecs[:, V_GLU:V_GLU + 8] = fm(glu_b[0], 8)
    vecs[:, V_GQ] = np.tile(np.asarray(q_norm_g[0], f32), 2)
    vecs[:, V_GK] = np.tile(np.asarray(k_norm_g[0], f32), 2)
    vecs[:, V_DWB:V_DWB + 4] = fm(dw_b[0], 4)
    vecs[:, V_LNG:V_LNG + 4] = fm(conv_ln_g[0], 4)
    vecs[:, V_LNB:V_LNB + 4] = fm(conv_ln_b[0], 4)
    dww = np.asarray(dw_w[0], f32).reshape(31, 4, 128).transpose(2, 1, 0).reshape(128, 4 * 31)
    dww = np.ascontiguousarray(dww)
    cbf, oneh = _consts()

    shared = {
        "w_in": np.ascontiguousarray(np.asarray(w_in[0], f32)),
        "w_out": np.ascontiguousarray(np.asarray(w_out[0], f32)),
        "w_gu": np.ascontiguousarray(np.concatenate(
            [np.asarray(w_gate[0], f32).reshape(D, NFF, 128), np.asarray(w_up[0], f32).reshape(D, NFF, 128)], axis=2).reshape(D, NFF * 256)),
        "w_down": np.ascontiguousarray(np.asarray(w_down[0], f32)),
        "dww": dww, "cbf": cbf, "oneh": oneh,
    }
    tabs = [_rope_tables(h) for h in range(2)]
    eligs = [_elig(h) for h in range(2)]
    in_maps = []
    for core in range(8):
        b, half = core // 2, core % 2
        xw = np.zeros((W, D), f32)
        if half == 1:
            xw[:] = x[b]
        else:
            xw[T:] = x[b, :T]
        v = vecs.copy()
        v[:, V_FLAG] = float(half)
        v[:, V_E1] = EPS
        v[:, V_E64] = 64.0 * EPS
        m = dict(shared)
        m.update({"xw": xw, "vecs": v, "cos_t": tabs[half][0], "sin_t": tabs[half][1],
                  "elig": eligs[half][0], "inel": eligs[half][1]})
        in_maps.append(m)

    if "nc" not in _NC_CACHE:
        _NC_CACHE["nc"] = build_program()
    nc = _NC_CACHE["nc"]
    res = run_bass_kernel_spmd(nc, in_maps, core_ids=list(range(8)))
    out = np.zeros((B, S, D), f32)
    for core in range(8):
        b, half = core // 2, core % 2
        out[b, half * T:(half + 1) * T] = res.results[core]["y"]
    if DEBUG:
        kernel.dbg = [res.results[c]["dbg"] for c in range(8)]
    return out
```

```python
import numpy as np
import ml_dtypes
from contextlib import ExitStack

import concourse.bass as bass
import concourse.mybir as mybir
from concourse.bass_utils import run_bass_kernel_spmd

F32 = mybir.dt.float32
BF16 = mybir.dt.bfloat16
ALU = mybir.AluOpType
ACT = mybir.ActivationFunctionType
AX = mybir.AxisListType

D = 1024
S = 4096
B = 4
T = 2048
W = 4096
DFF = 2816
NFF = DFF // 128
EPS = 1e-6
NEG = -32768.0
BIGF = 1.0e30
DEBUG = False

NDSEM = 8
ATTACH_WAIT = True


class _PEProxy:
    def __init__(self, e):
        self.e = e
        self.first = None

    def matmul(self, *a, **k):
        ins = self.e.matmul(*a, **k)
        if self.first is None:
            self.first = ins
        return ins


class Op:
    __slots__ = ("eng", "fn", "isdma", "deps", "raw", "cost", "xfer", "uid", "pos", "sem", "val", "waits", "bl", "nsucc", "succ")


DEF_COST = {"pe": 0.3, "act": 0.65, "dve": 0.7, "pool": 1.3, "sp": 0.1}
HOP = 0.2


class Prog:
    COMPUTE = ("pe", "act", "dve", "pool")
    ENGS = ("pe", "act", "dve", "pool", "sp")

    def __init__(self, nc):
        self.nc = nc
        self.cur = []
        self.count = {e: 0 for e in self.COMPUTE}
        self.dcount = {"sp": 0, "pool": 0}
        self.last_writer = {}
        self.readers = {}
        self.waited = {}
        self.sems = {}
        self.dma_final = {}
        self.barrier_snap = None
        self.uid = 0

    def alloc_sems(self, stack):
        nc = self.nc
        for e in self.COMPUTE:
            self.sems[e] = stack.enter_context(nc.semaphore("c_" + e))
        for q in ("sp", "pool"):
            for i in range(NDSEM):
                self.sems[(q, i)] = stack.enter_context(nc.semaphore("d_%s%d" % (q, i)))

    def barrier(self):
        assert not self.cur
        snap = []
        for e in self.COMPUTE:
            if self.count[e] > 0:
                snap.append((e, self.count[e]))
        for k, v in self.dma_final.items():
            snap.append((k, v))
        self.barrier_snap = snap
        self.last_writer = {}
        self.readers = {}

    def op(self, eng, fn, reads=(), writes=(), dma=False, cost=None, xfer=3.0):
        o = Op()
        o.eng = eng
        o.fn = fn
        o.isdma = dma
        o.cost = (0.1 if dma else DEF_COST[eng]) if cost is None else cost
        o.xfer = xfer if dma else 0.0
        o.uid = self.uid
        self.uid += 1
        deps = set()
        raw = set()
        for k in reads:
            w = self.last_writer.get(k)
            if w is not None:
                deps.add(w)
                raw.add(w)
        for k in writes:
            w = self.last_writer.get(k)
            if w is not None:
                deps.add(w)
            deps.update(self.readers.get(k, ()))
        deps.discard(o)
        o.deps = deps
        o.raw = raw
        for k in reads:
            self.readers.setdefault(k, []).append(o)
        for k in writes:
            self.last_writer[k] = o
            self.readers[k] = []
        self.cur.append(o)
        return o

    def _schedule(self, ops):
        import heapq
        inphase = set(id(o) for o in ops)
        for o in ops:
            o.deps = [d for d in o.deps if id(d) in inphase]
            o.succ = []
        for o in ops:
            for d in o.deps:
                d.succ.append(o)
        for o in reversed(ops):
            m = 0.0
            for s_ in o.succ:
                if s_.bl > m:
                    m = s_.bl
            o.bl = o.cost + o.xfer + m
        ndep = {id(o): len(o.deps) for o in ops}
        finish = {}
        ready = {e: [] for e in self.ENGS}
        for o in ops:
            if ndep[id(o)] == 0:
                heapq.heappush(ready[o.eng], (-o.bl, o.uid, o))
        free = {e: 0.0 for e in self.ENGS}
        order = {e: [] for e in self.ENGS}
        nleft = len(ops)

        def dready(o):
            t = 0.0
            for d in o.deps:
                f = finish[id(d)] + (HOP if (d.eng != o.eng or d.isdma) else 0.0)
                if f > t:
                    t = f
            return t
        while nleft:
            best = None
            for e in self.ENGS:
                h = ready[e]
                if not h:
                    continue
                cands = heapq.nsmallest(6, h)
                pick = None
                pick_t = None
                for c in cands:
                    t = max(free[e], dready(c[2]))
                    if t <= free[e] + 1e-9:
                        pick, pick_t = c, t
                        break
                    if pick is None or t < pick_t:
                        pick, pick_t = c, t
                if best is None or pick_t < best[0]:
                    best = (pick_t, e, pick)
            t0, e, c = best
            ready[e].remove(c)
            heapq.heapify(ready[e])
            o = c[2]
            free[e] = t0 + o.cost
            finish[id(o)] = t0 + o.cost + o.xfer
            order[e].append(o)
            nleft -= 1
            for s_ in o.succ:
                ndep[id(s_)] -= 1
                if ndep[id(s_)] == 0:
                    heapq.heappush(ready[s_.eng], (-s_.bl, s_.uid, s_))
        return order

    def emit(self, block, final_wait=False):
        ops = self.cur
        self.cur = []
        order = self._schedule(ops)
        sems = self.sems
        for e in self.ENGS:
            for i, o in enumerate(order[e]):
                o.pos = i
                if o.isdma:
                    n = self.dcount[e]
                    self.dcount[e] += 1
                    o.sem = (e, n % NDSEM)
                    o.val = 16 * (n // NDSEM + 1)
                    self.dma_final[o.sem] = o.val
                else:
                    self.count[e] += 1
                    o.sem = e
                    o.val = self.count[e]
        for e in self.ENGS:
            first = True
            for o in order[e]:
                o.waits = []

                def addw(sk, v, o=o, e=e):
                    k = (e, sk)
                    if self.waited.get(k, 0) >= v:
                        return
                    self.waited[k] = v
                    o.waits.append((sk, v))
                if first and self.barrier_snap:
                    for (sk, v) in self.barrier_snap:
                        if sk == e and not o.isdma:
                            continue
                        addw(sk, v)
                first = False
                for d in o.deps:
                    if (not d.isdma) and d.eng == e and not o.isdma:
                        if e == "pe":
                            continue
                        if o.pos - d.pos > 2:
                            if self.waited.get((e, e), 0) >= d.val:
                                continue
                            v = d.val
                            for back in range(3, 8):
                                if o.pos - back < 0:
                                    break
                                q = order[e][o.pos - back]
                                if not q.isdma:
                                    v = max(v, q.val)
                                    break
                            addw(e, v)
                            continue
                    addw(d.sem, d.val)
                if o.isdma and o.val > 16:
                    addw(o.sem, o.val - 16)
                if len(o.waits) > 1:
                    mx = {}
                    for (sk, v) in o.waits:
                        if mx.get(sk, 0) < v:
                            mx[sk] = v
                    o.waits = list(mx.items())
        self.barrier_snap = None
        final = list(self.dma_final.items()) if final_wait else []

        def run(engname, e):
            for o in order[engname]:
                ws = list(o.waits)
                att = None
                if ATTACH_WAIT and ws and not o.isdma:
                    att = ws.pop()
                for (sk, v) in ws:
                    e.wait_ge(sems[sk], v)
                if engname == "pe":
                    px = _PEProxy(e)
                    ins = o.fn(px)
                    first = px.first
                else:
                    ins = o.fn(e)
                    first = ins
                if att is not None:
                    first._wait_ge(sems[att[0]], att[1])
                if o.isdma:
                    ins.then_inc(sems[o.sem], 16)
                else:
                    ins.then_inc(sems[o.sem], 1)
            if engname == "sp":
                for (sk, v) in final:
                    e.wait_ge(sems[sk], v)

        @block.tensor
        def _(e):
            run("pe", e)

        @block.scalar
        def _(e):
            run("act", e)

        @block.vector
        def _(e):
            run("dve", e)

        @block.gpsimd
        def _(e):
            run("pool", e)

        @block.sync
        def _(e):
            run("sp", e)


NPTS = [8, 5]
WIDE_RING = True
XN_ACT = False
NHT = [2, 2]
NSCR = [2, 2]
NCS = [2, 2]
NXT = [4, 4]
V_G1, V_G2, V_GLU, V_GQ, V_GK, V_DWB, V_LNG, V_LNB, V_FLAG, V_E1, V_E64 = 0, 8, 16, 24, 25, 26, 30, 34, 38, 39, 40


def build_program():
    nc = bass.Bass("TRN2", target_bir_lowering=False)

    def din(name, shape, dt=F32):
        return nc.dram_tensor(name, list(shape), dt, kind="ExternalInput").ap()

    xw = din("xw", [W, D])
    w_in = din("w_in", [D, 2560])
    w_out = din("w_out", [D, D])
    w_gu = din("w_gu", [D, NFF * 256])
    w_down = din("w_down", [DFF, D])
    vecs_d = din("vecs", [128, 64])
    dww_d = din("dww", [128, 4 * 31])
    cos_d = din("cos_t", [128, W])
    sin_d = din("sin_t", [128, W])
    elig_d = din("elig", [128, 256])
    inel_d = din("inel", [128, 256])
    cbf_d = din("cbf", [128, 5 * 128], BF16)
    oneh_d = din("oneh", [16, W], BF16)
    y = nc.dram_tensor("y", [T, D], F32, kind="ExternalOutput").ap()
    dbg = None
    if DEBUG:
        dbg = nc.dram_tensor("dbg", [128, 8 * T], BF16, kind="ExternalOutput").ap()

    with ExitStack() as outer:
        P = Prog(nc)
        P.alloc_sems(outer)

        def SB(st, name, shape, dt):
            return st.enter_context(nc.sbuf_tensor(name, list(shape), dt))

        ps = [outer.enter_context(nc.psum_tensor("psb%d" % i, [128, 512], F32)) for i in range(8)]
        psname = {id(p): i for i, p in enumerate(ps)}
        ring_state = {}

        def bank(group, banks):
            i = ring_state.get(group, 0)
            ring_state[group] = i + 1
            return banks[i % len(banks)]

        def pk(p):
            return ("ps", psname[id(p)])

        mixc = [None] * 8
        for c_ in (0, 1):
            mixc[c_] = SB(outer, "mixT%d" % c_, [128, T], BF16)
        vecs = SB(outer, "vecs_sb", [128, 64], F32)
        dww = SB(outer, "dww_sb", [128, 4 * 31], F32)
        cbf = SB(outer, "cbf_sb", [128, 5 * 128], BF16)
        elig = SB(outer, "elig_sb", [128, 256], F32)
        inel = SB(outer, "inel_sb", [128, 256], F32)
        ident = cbf[:, 0:128]
        tri = cbf[:, 128:256]
        blockones = cbf[:, 256:384]
        perm = cbf[:, 384:512]
        allones = cbf[:, 512:640]

        P.op("sp", lambda e: e.dma_start(out=vecs[:], in_=vecs_d), writes=["vecs"], dma=True)
        P.op("sp", lambda e: e.dma_start(out=dww[:], in_=dww_d), writes=["dww"], dma=True)
        P.op("sp", lambda e: e.dma_start(out=cbf[:], in_=cbf_d), writes=["cbf"], dma=True)
        P.op("sp", lambda e: e.dma_start(out=elig[:], in_=elig_d), writes=["elig"], dma=True)
        P.op("sp", lambda e: e.dma_start(out=inel[:], in_=inel_d), writes=["inel"], dma=True)

        def vcol(c):
            return vecs[:, c:c + 1]

        V_HBG = 41
        P.op("dve", lambda e: e.tensor_scalar(out=vecs[:, V_HBG:V_HBG + 4], in0=vecs[:, V_GLU + 4:V_GLU + 8], scalar1=0.5, scalar2=None, op0=ALU.mult),
             reads=["vecs"], writes=["vecs"])
        rstd_all = SB(outer, "rstd_all", [128, 32], F32)
        cneg = SB(outer, "cneg", [128, 2], F32)
        P.op("pool", lambda e: e.memset(cneg[:, 0:1], -1.0), writes=["cneg0"])
        P.op("pool", lambda e: e.memset(cneg[:, 1:2], -0.5), writes=["cneg1"])

        def cbc(k, lo, hi, n):
            return bass.AP(cneg, lo * 2 + k, [[2, hi - lo], [0, n]])

        G3 = [ps[5], ps[6], ps[7]]
        SR = [ps[0], ps[1], ps[2]]
        OR = [ps[3], ps[4]]

        def issue_w(pas, wsb, which=None, after=()):
            wsrc = [(256 * pas, 256), (512 + 256 * pas, 256), (1024 + 256 * pas, 256)]
            if pas == 0:
                wsrc.append((1536, 1024))
            segs = []
            off = 0
            offs = []
            for (c0, n) in wsrc:
                segs.append((off, n))
                offs.append(off)
                off += n
            if which is None:
                which = [1, 2, 0, 3] if pas == 0 else [1, 2, 0]
            for si in which:
                (c0, n) = wsrc[si]
                off = offs[si]
                for hk in range(2):
                    def f(e, c0=c0, n=n, hk=hk, off=off):
                        return e.dma_start(out=wsb[:, 4 * hk:4 * hk + 4, off:off + n],
                                           in_=w_in[512 * hk:512 * hk + 512, c0:c0 + n].rearrange("(c p) n -> p c n", p=128))
                    P.op("pool", f, reads=list(after), writes=[("wsb", pas, off, kc) for kc in range(4 * hk, 4 * hk + 4)], dma=True,
                         xfer=(6.0 if n == 256 else 20.0))
            return segs

        hscope = ExitStack()
        w1scope = ExitStack()
        hglu = hscope.enter_context(nc.sbuf_tensor("hglu", [128, 4, 32 + T], BF16, side="right"))

        for pas in range(2):
            with ExitStack() as sc:
                Kaug = [SB(sc, "Kaug%d_%d" % (pas, h), [128, W], BF16) for h in range(4)]
                Vaug = SB(sc, "Vaug%d" % pas, [128, 32, 384], BF16)
                Qaug = [[SB(sc, "Qaug%d_%d_%d" % (pas, b_, h), [128, 512], BF16) for h in range(4)] for b_ in range(2)]
                if pas == 0:
                    wsb = SB(sc, "wsb0", [128, 8, 1792], BF16)
                else:
                    wsb = wsb1
                xts = [SB(sc, "xt%d_%d" % (pas, i), [128, D], F32) for i in range(NXT[pas])]
                xn4 = SB(sc, "xn4_%d" % pas, [128, 4, D], BF16)
                ssq = [SB(sc, "ssq%d_%d" % (pas, i), [128, 1], F32) for i in range(NXT[pas])]
                rstd1 = [SB(sc, "rstd%d_%d" % (pas, i), [128, 1], F32) for i in range(NXT[pas])]
                hTs = [SB(sc, "hT%d_%d" % (pas, i), [128, 8, 512], BF16) for i in range(NHT[pas])]
                cosb = [SB(sc, "cos%d_%d" % (pas, i), [128, 512], F32) for i in range(NCS[pas])]
                sinb = [SB(sc, "sin%d_%d" % (pas, i), [128, 512], F32) for i in range(NCS[pas])]
                SCR = []
                for i_ in range(NSCR[pas]):
                    SCR.append(dict(
                        sq=SB(sc, "sq%d_%d" % (pas, i_), [128, 512], BF16),
                        rs=SB(sc, "rs%d_%d" % (pas, i_), [128, 512], F32),
                        rc=SB(sc, "rc%d_%d" % (pas, i_), [128, 512], F32),
                        qnb=SB(sc, "qnb%d_%d" % (pas, i_), [128, 512], BF16),
                        t1=SB(sc, "t1_%d_%d" % (pas, i_), [128, 512], F32),
                        t2=SB(sc, "t2_%d_%d" % (pas, i_), [128, 512], F32),
                        ksum=SB(sc, "ksum%d_%d" % (pas, i_), [128, 4], F32)))
                scr_i = [0]
                kmean = [SB(sc, "kmean%d_%d" % (pas, h), [128, 16], BF16) for h in range(4)]
                pT = [SB(sc, "pT%d_%d" % (pas, i), [128, 512], BF16) for i in range(NPTS[pas])]
                NPT = len(pT)
                rd = [SB(sc, "rd%d_%d" % (pas, i), [128, 512], F32) for i in range(2)]
                Gm = SB(sc, "Gm%d" % pas, [128, 64], F32)
                top8 = SB(sc, "top8_%d" % pas, [128, 32], F32)
                Bt = SB(sc, "Bt%d" % pas, [128, 4, 128], BF16)
                sig = SB(sc, "sig%d" % pas, [128, 512], F32) if pas == 0 else None

                segs = issue_w(pas, wsb, which=[1, 2]) if pas == 0 else segs1
                WQ, WK, WV, WU = 0, 256, 512, 768
                hb_ = [0]
                gr_ = ["g", G3]
                seq_ = [0]
                cbk_ = [0]

                def wread(off_, n):
                    return [("wsb", pas, o2, kc) for (o2, nn) in segs if off_ < o2 + nn and off_ + n > o2 for kc in range(8)]

                for q4 in range(4):
                    P.op("pool", lambda e, q4=q4: e.memset(
                        Vaug[:, 8 * q4:8 * q4 + 8, :].rearrange("p k (a b) -> p k a b", a=2)[:, :, :, 64:128], 1.0),
                        writes=[("Vaug_ones", q4)], cost=1.0)
                P.op("pool", lambda e: e.memset(Bt[:, :, :], 0.0), writes=["Bt"])
                for h in range(4):
                    P.op("pool", lambda e, h=h: e.memset(kmean[h][:], 0.0), writes=[("kmean", h)])
                if pas == 0:
                    P.op("pool", lambda e: e.memset(hglu[:, :, 0:32], 0.0), writes=["hglu_halo"])

                def qk_post(pb, is_q, wb, pair):
                    cb = cbk_[0]
                    hA, hB = 2 * pair, 2 * pair + 1
                    gcol = V_GQ if is_q else V_GK
                    si = scr_i[0] % len(SCR)
                    scr_i[0] += 1
                    S_ = SCR[si]
                    sq, rs, rc, qnb, t1, t2, ksum = S_["sq"], S_["rs"], S_["rc"], S_["qnb"], S_["t1"], S_["t2"], S_["ksum"]
                    sd = rs

                    def Y(k):
                        for _ in range(k):
                            yield
                    P.op("act", lambda e: e.activation(out=sq[:], in_=pb[:], func=ACT.Square),
                         reads=[pk(pb)], writes=[("sq", si)], cost=0.6)
                    yield from Y(2)
                    pb2 = bank(gr_[0], gr_[1])
                    P.op("pe", lambda e: e.matmul(pb2[:], lhsT=blockones, rhs=sq[:], start=True, stop=True),
                         reads=[("sq", si), "cbf"], writes=[pk(pb2)], cost=0.25)
                    yield from Y(2)
                    if is_q:
                        P.op("act", lambda e: e.activation(out=sd[:], in_=pb2[:], func=ACT.Ln, scale=1.0, bias=vcol(V_E64)),
                             reads=[pk(pb2), "vecs"], writes=[("rs", si)])
                    else:
                        P.op("act", lambda e: e.activation(out=sd[:], in_=pb2[:], func=ACT.Ln, scale=1.0 / 64.0, bias=vcol(V_E1)),
                             reads=[pk(pb2), "vecs"], writes=[("rs", si)])
                    P.op("act", lambda e: e.activation(out=rs[:], in_=sd[:], func=ACT.Exp, scale=-0.5),
                         reads=[("rs", si)], writes=[("rs", si)])
                    yield from Y(3)
                    P.op("dve", lambda e: e.scalar_tensor_tensor(out=qnb[:], in0=pb[:], scalar=vcol(gcol), in1=rs[:],
                                                                   op0=ALU.mult, op1=ALU.mult),
                         reads=[pk(pb), ("rs", si), "vecs"], writes=[("qnb", si)])
                    P.op("pool", lambda e: e.tensor_tensor(out=rc[:], in0=rs[:], in1=cosb[cb][:], op=ALU.mult),
                         reads=[("rs", si), ("cos", cb)], writes=[("rc", si)], cost=1.3)
                    yield from Y(3)
                    pb3 = bank(gr_[0], gr_[1])
                    P.op("pe", lambda e: e.matmul(pb3[:], lhsT=perm, rhs=qnb[:], start=True, stop=True),
                         reads=[("qnb", si), "cbf"], writes=[pk(pb3)], cost=0.25)
                    yield from Y(2)
                    P.op("dve", lambda e: e.scalar_tensor_tensor(out=t1[:], in0=pb[:], scalar=vcol(gcol), in1=rc[:],
                                                                   op0=ALU.mult, op1=ALU.mult),
                         reads=[pk(pb), ("rc", si), "vecs"], writes=[("t1", si)])
                    P.op("dve", lambda e: e.tensor_tensor(out=t2[:], in0=pb3[:], in1=sinb[cb][:], op=ALU.mult),
                         reads=[pk(pb3), ("sin", cb)], writes=[("t2", si)])
                    yield from Y(3)
                    if is_q:
                        qb = wb % 2
                        P.op("pool", lambda e: e.tensor_tensor(out=Qaug[qb][hA][0:64, :], in0=t1[0:64, :], in1=t2[0:64, :], op=ALU.add),
                             reads=[("t1", si), ("t2", si)], writes=[("Qaug", qb, hA)])
                        P.op("pool", lambda e: e.tensor_tensor(out=Qaug[qb][hB][0:64, :], in0=t1[64:128, :], in1=t2[64:128, :], op=ALU.add),
                             reads=[("t1", si), ("t2", si)], writes=[("Qaug", qb, hB)])
                    else:
                        c0 = wb * 512
                        P.op("pool", lambda e: e.tensor_tensor(out=Kaug[hA][0:64, c0:c0 + 512], in0=t1[0:64, :], in1=t2[0:64, :], op=ALU.add),
                             reads=[("t1", si), ("t2", si)], writes=[("Kaug", hA, wb)])
                        P.op("pool", lambda e: e.tensor_tensor(out=Kaug[hB][0:64, c0:c0 + 512], in0=t1[64:128, :], in1=t2[64:128, :], op=ALU.add),
                             reads=[("t1", si), ("t2", si)], writes=[("Kaug", hB, wb)])
                        P.op("dve", lambda e: e.tensor_reduce(out=ksum[:, 0:2], in_=t1[:].rearrange("p (b k) -> p b k", b=2),
                                                               axis=AX.X, op=ALU.add),
                             reads=[("t1", si)], writes=[("ksum1", si)])
                        P.op("dve", lambda e: e.tensor_reduce(out=ksum[:, 2:4], in_=t2[:].rearrange("p (b k) -> p b k", b=2),
                                                               axis=AX.X, op=ALU.add),
                             reads=[("t2", si)], writes=[("ksum2", si)])
                        yield from Y(3)
                        P.op("dve", lambda e: e.tensor_tensor(out=kmean[hA][0:64, 2 * wb:2 * wb + 2], in0=ksum[0:64, 0:2], in1=ksum[0:64, 2:4], op=ALU.add),
                             reads=[("ksum1", si), ("ksum2", si)], writes=[("kmean", hA)], cost=0.2)
                        P.op("dve", lambda e: e.tensor_tensor(out=kmean[hB][0:64, 2 * wb:2 * wb + 2], in0=ksum[64:128, 0:2], in1=ksum[64:128, 2:4], op=ALU.add),
                             reads=[("ksum1", si), ("ksum2", si)], writes=[("kmean", hB)], cost=0.2)
                    yield

                def proj_fm(off_, ntok, tok0):
                    pb = bank(gr_[0], gr_[1])
                    hbi = hb_[0]
                    hT = hTs[hbi]

                    def f(e):
                        ins = None
                        for kc in range(8):
                            ins = e.matmul(pb[:, 0:ntok], lhsT=wsb[:, kc, off_:off_ + 128],
                                           rhs=hT[:, kc, tok0:tok0 + ntok], start=(kc == 0), stop=(kc == 7))
                        return ins
                    P.op("pe", f, reads=wread(off_, 128) + [("hT", hbi, c) for c in range(8)], writes=[pk(pb)], cost=(1.8 if ntok == 512 else 0.6))
                    return pb

                def gating(qc):
                    qb = qc % 2
                    if qc == 0 and WIDE_RING:
                        gr_[0], gr_[1] = "g8", list(ps)
                    elif qc == 0:
                        gr_[0], gr_[1] = "g", G3
                    for t in range(4):
                        qt = 4 * qc + t
                        gp = bank(gr_[0], gr_[1])

                        def fG(e, gp=gp, t=t):
                            ins = None
                            for h in range(4):
                                ins = e.matmul(gp[:, 16 * h:16 * h + 16], lhsT=Qaug[qb][h][0:64, 128 * t:128 * t + 128],
                                               rhs=kmean[h][0:64, :], start=True, stop=True)
                            return ins
                        P.op("pe", fG, reads=[("Qaug", qb, h) for h in range(4)] + [("kmean", h) for h in range(4)],
                             writes=[pk(gp)], cost=0.25)
                        yield
                        yield
                        for h in range(4):
                            P.op("dve", lambda e, gp=gp, h=h, qt=qt: e.tensor_tensor(
                                out=Gm[:, 16 * h:16 * h + 16], in0=gp[:, 16 * h:16 * h + 16],
                                in1=elig[:, 16 * qt:16 * qt + 16], op=ALU.add),
                                reads=[pk(gp), "elig"], writes=[("Gm", h)], cost=0.25)
                            P.op("dve", lambda e, h=h: e.max(out=top8[:, 8 * h:8 * h + 8], in_=Gm[:, 16 * h:16 * h + 16]),
                                 reads=[("Gm", h)], writes=[("top8", h)], cost=0.3)
                            P.op("dve", lambda e, h=h, qt=qt, t=t: e.scalar_tensor_tensor(
                                out=Bt[:, t, 64 + 16 * h:64 + 16 * h + 16], in0=Gm[:, 16 * h:16 * h + 16],
                                scalar=top8[:, 8 * h + 3:8 * h + 4], in1=inel[:, 16 * qt:16 * qt + 16],
                                op0=ALU.is_lt, op1=ALU.add),
                                reads=[("Gm", h), ("top8", h), "inel", "Bt"], writes=[("Bt", t, h)], cost=0.3)
                        yield
                    for h in range(4):
                        bt = bank(gr_[0], gr_[1])

                        def fT(e, h=h, bt=bt):
                            ins = None
                            for t in range(4):
                                ins = e.matmul(bt[0:80, 128 * t:128 * t + 128], lhsT=Bt[:, t, 16 * h:16 * h + 80],
                                               rhs=ident, start=True, stop=True)
                            return ins
                        P.op("pe", fT, reads=[("Bt", t, hh) for t in range(4) for hh in range(4)] + ["Bt", "cbf"],
                             writes=[pk(bt)], cost=0.45)
                        yield
                        yield
                        P.op("dve", lambda e, h=h, bt=bt: e.tensor_copy(out=Qaug[qb][h][64:80, :], in_=bt[64:80, :]),
                             reads=[pk(bt)], writes=[("Qbias", qb, h)], cost=0.45)
                        yield

                def attention(qc):
                    qb = qc % 2
                    nfull = 16 + 4 * qc
                    for h in range(4):
                        pair = h // 2
                        isB = h % 2
                        po = bank("o", OR)
                        steps = []
                        for kt in range(nfull + 4):
                            i = kt - nfull
                            steps.append((kt, 0 if i < 0 else 128 * i, i))
                        n = len(steps)
                        LAG = 2
                        sbank = {}
                        vlo = pair * 192 + (64 if isB else 0)
                        for j in range(n + LAG):
                            if j < n:
                                kt, q0, i = steps[j]
                                sb_ = bank("s", SR)
                                sbank[j] = sb_

                                def fS(e, kt=kt, q0=q0, i=i, sb_=sb_, h=h):
                                    ins = e.matmul(sb_[:, q0:512], lhsT=Kaug[h][0:80, kt * 128:(kt + 1) * 128],
                                                   rhs=Qaug[qb][h][0:80, q0:512], start=True, stop=(i < 0))
                                    if i >= 0:
                                        ins = e.matmul(sb_[:, q0:q0 + 128], lhsT=ident, rhs=tri, start=False, stop=True)
                                    return ins
                                P.op("pe", fS, reads=[("Kaug", h, kt // 4), ("Kaug_oh", h), ("Qaug", qb, h), ("Qbias", qb, h), "cbf"],
                                     writes=[pk(sb_)], cost=(0.24 if i < 0 else 0.32))
                            if j >= LAG:
                                jj = j - LAG
                                kt, q0, i = steps[jj]
                                sb_ = sbank.pop(jj)
                                pt = pT[jj % NPT]
                                P.op("act", lambda e, sb_=sb_, pt=pt, q0=q0: e.activation(
                                    out=pt[:, q0:512], in_=sb_[:, q0:512], func=ACT.Exp),
                                    reads=[pk(sb_)], writes=[("pT", jj % NPT)], cost=0.5)

                                def fO(e, kt=kt, q0=q0, pt=pt, jj=jj, po=po, vlo=vlo, n=n):
                                    return e.matmul(po[:, q0:512], lhsT=Vaug[:, kt, vlo:vlo + 128], rhs=pt[:, q0:512],
                                                    start=(jj == 0), stop=(jj == n - 1))
                                P.op("pe", fO, reads=[("pT", jj % NPT), ("Vaug", kt), ("Vaug_ones", kt // 8)], writes=[pk(po)], cost=0.23)
                            yield
                        r_ = rd[h % 2]
                        cch = 2 * pas + pair
                        c0 = qc * 512
                        lo, hi = (0, 64) if not isB else (64, 128)
                        dl, dh = (64, 128) if not isB else (0, 64)
                        P.op("dve", lambda e, r_=r_, po=po, dl=dl, dh=dh: e.reciprocal(out=r_[dl:dh, :], in_=po[dl:dh, :]),
                             reads=[pk(po)], writes=[("rd", h % 2)], cost=3.4)
                        P.op("dve", lambda e, r_=r_, po=po, cch=cch, c0=c0, lo=lo, hi=hi, dl=dl, dh=dh: e.tensor_tensor(
                            out=mixc[cch][lo:hi, c0:c0 + 512], in0=po[lo:hi, :], in1=r_[dl:dh, :], op=ALU.mult),
                            reads=[pk(po), ("rd", h % 2)], writes=[("mixT", cch, qc, isB)])

                def prep_gen(wb, do_gating=True):
                    own = wb >= 4
                    sq_ = seq_[0]
                    seq_[0] += 1
                    if wb <= 4 and WIDE_RING:
                        gr_[0], gr_[1] = "g8", list(ps)
                    else:
                        gr_[0], gr_[1] = "g", G3
                    cb = sq_ % len(cosb)
                    cbk_[0] = cb
                    hT = hTs[sq_ % len(hTs)]
                    hb_[0] = sq_ % len(hTs)
                    P.op("sp", lambda e, wb=wb, cb=cb: e.dma_start(out=cosb[cb][:], in_=cos_d[:, wb * 512:(wb + 1) * 512]),
                         writes=[("cos", cb)], dma=True)
                    P.op("sp", lambda e, wb=wb, cb=cb: e.dma_start(out=sinb[cb][:], in_=sin_d[:, wb * 512:(wb + 1) * 512]),
                         writes=[("sin", cb)], dma=True)
                    for t in range(4):
                        wt = 4 * wb + t
                        xb = wt % len(xts)
                        P.op("sp", lambda e, wt=wt, xb=xb: e.dma_start(out=xts[xb][:], in_=xw[wt * 128:(wt + 1) * 128, :]),
                             writes=[("xt", xb)], dma=True)
                        if pas == 0:
                            P.op("act", lambda e, xb=xb, t=t: e.activation(out=xn4[:, t, :], in_=xts[xb][:], func=ACT.Square, accum_out=ssq[xb][:]),
                                 reads=[("xt", xb)], writes=[("xn", t), ("ssq", xb)], cost=1.05)
                            P.op("dve", lambda e, xb=xb: e.tensor_scalar(out=rstd1[xb][:], in0=ssq[xb][:], scalar1=1.0 / D, scalar2=EPS, op0=ALU.mult, op1=ALU.add),
                                 reads=[("ssq", xb)], writes=[("rstd", xb)], cost=0.2)
                            P.op("pool", lambda e, xb=xb, wt=wt: e.tensor_tensor(out=rstd_all[:, wt:wt + 1], in0=rstd1[xb][:], in1=cneg[:, 1:2], op=ALU.pow),
                                 reads=[("rstd", xb), "cneg1"], writes=[("rstdall", wt)], cost=0.5)
                        if pas == 1 and wb <= 4 and XN_ACT:
                            P.op("act", lambda e, xb=xb, t=t, wt=wt: e.activation(out=xn4[:, t, :], in_=xts[xb][:], func=ACT.Copy, scale=rstd_all[:, wt:wt + 1]),
                                 reads=[("xt", xb), ("rstdall", wt)], writes=[("xn", t)], cost=1.25)
                        else:
                            P.op("dve", lambda e, xb=xb, t=t, wt=wt: e.tensor_scalar(out=xn4[:, t, :], in0=xts[xb][:], scalar1=rstd_all[:, wt:wt + 1],
                                                                                     scalar2=None, op0=ALU.mult),
                                 reads=[("xt", xb), ("rstdall", wt)], writes=[("xn", t)], cost=0.8)
                        yield
                    for c in range(8):
                        tp = bank(gr_[0], gr_[1])

                        def fTr(e, tp=tp, c=c):
                            ins = None
                            for t in range(4):
                                ins = e.matmul(tp[:, 128 * t:128 * t + 128], lhsT=xn4[:, t, 128 * c:128 * c + 128],
                                               rhs=ident, start=True, stop=True)
                            return ins
                        P.op("pe", fTr, reads=[("xn", t) for t in range(4)] + ["cbf"], writes=[pk(tp)], cost=0.5)
                        if (c % 2 == 0) if wb <= 4 else (c % 4 != 3):
                            P.op("dve", lambda e, tp=tp, c=c: e.tensor_scalar(
                                out=hT[:, c, :], in0=tp[:], scalar1=vcol(V_G1 + c), scalar2=None, op0=ALU.mult),
                                reads=[pk(tp), "vecs"], writes=[("hT", sq_ % len(hTs), c)], cost=0.75)
                        else:
                            P.op("act", lambda e, tp=tp, c=c: e.activation(
                                out=hT[:, c, :], in_=tp[:], func=ACT.Copy, scale=vcol(V_G1 + c)),
                                reads=[pk(tp), "vecs"], writes=[("hT", sq_ % len(hTs), c)])
                        yield
                    for pair in range(2):
                        pb = proj_fm(WK + 128 * pair, 512, 0)
                        yield
                        for _ in qk_post(pb, False, wb, pair):
                            yield
                    for t in range(4):
                        kt = 4 * wb + t
                        pb = bank(gr_[0], gr_[1])

                        def fV(e, pb=pb, t=t):
                            ins = None
                            for kc in range(8):
                                ins = e.matmul(pb[:, 0:256], lhsT=hT[:, kc, 128 * t:128 * t + 128], rhs=wsb[:, kc, WV:WV + 256],
                                               start=(kc == 0), stop=(kc == 7))
                            return ins
                        P.op("pe", fV, reads=wread(WV, 256) + [("hT", sq_ % len(hTs), c) for c in range(8)], writes=[pk(pb)], cost=1.0)
                        vdst = Vaug[:, kt, :].rearrange("p (a b) -> p a b", a=2)
                        vsrc = pb[:, 0:256].rearrange("p (a b) -> p a b", a=2)
                        if wb <= 4:
                            P.op("act", lambda e, vdst=vdst, vsrc=vsrc: e.copy(out=vdst[:, :, 0:64], in_=vsrc[:, :, 0:64]),
                                 reads=[pk(pb), ("Vaug_ones", kt // 8)], writes=[("Vaug", kt)], cost=0.45)
                            P.op("act", lambda e, vdst=vdst, vsrc=vsrc: e.copy(out=vdst[:, :, 128:192], in_=vsrc[:, :, 64:128]),
                                 reads=[pk(pb), ("Vaug_ones", kt // 8)], writes=[("Vaug", kt)], cost=0.45)
                        else:
                            P.op("dve", lambda e, vdst=vdst, vsrc=vsrc: e.tensor_copy(out=vdst[:, :, 0:64], in_=vsrc[:, :, 0:64]),
                                 reads=[pk(pb), ("Vaug_ones", kt // 8)], writes=[("Vaug", kt)], cost=0.35)
                            P.op("dve", lambda e, vdst=vdst, vsrc=vsrc: e.tensor_copy(out=vdst[:, :, 128:192], in_=vsrc[:, :, 64:128]),
                                 reads=[pk(pb), ("Vaug_ones", kt // 8)], writes=[("Vaug", kt)], cost=0.35)
                        yield
                    if own:
                        for pair in range(2):
                            pb = proj_fm(WQ + 128 * pair, 512, 0)
                            yield
                            for _ in qk_post(pb, True, wb, pair):
                                yield
                        for _ in range(4):
                            yield
                        if do_gating:
                            for _ in gating(wb - 4):
                                yield
                    if pas == 0 and (own or wb == 3):
                        for c in range(4):
                            if own:
                                ntok, tok0 = 512, 0
                            else:
                                ntok, tok0 = 128, 384
                            pa = proj_fm(WU + 128 * c, ntok, tok0)
                            pg = proj_fm(WU + 512 + 128 * c, ntok, tok0)
                            P.op("act", lambda e, pg=pg, c=c, ntok=ntok: e.activation(
                                out=sig[:, 0:ntok], in_=pg[:, 0:ntok], func=ACT.Tanh, bias=vcol(V_HBG + c), scale=0.5),
                                reads=[pk(pg), "vecs"], writes=["sig"], cost=0.7)
                            P.op("dve", lambda e, pa=pa, c=c, ntok=ntok: e.scalar_tensor_tensor(
                                out=sig[:, 0:ntok], in0=pa[:, 0:ntok], scalar=vcol(V_GLU + c), in1=sig[:, 0:ntok],
                                op0=ALU.add, op1=ALU.mult),
                                reads=[pk(pa), "sig", "vecs"], writes=["sig"], cost=0.75)
                            if own:
                                dst0 = 32 + (wb - 4) * 512
                                P.op("dve", lambda e, pa=pa, c=c, dst0=dst0: e.scalar_tensor_tensor(
                                    out=hglu[:, c, dst0:dst0 + 512], in0=pa[:, 0:512], scalar=vcol(V_GLU + c), in1=sig[:, 0:512],
                                    op0=ALU.add, op1=ALU.add),
                                    reads=[pk(pa), "sig", "vecs"], writes=[("hglu", c)], cost=0.75)
                            else:
                                P.op("dve", lambda e, pa=pa, c=c: e.scalar_tensor_tensor(
                                    out=hglu[:, c, 0:32], in0=pa[:, 96:128], scalar=vcol(V_GLU + c), in1=sig[:, 96:128],
                                    op0=ALU.add, op1=ALU.add),
                                    reads=[pk(pa), "sig", "vecs", "hglu_halo"], writes=[("hglu", c)], cost=0.3)
                                P.op("dve", lambda e, c=c: e.tensor_scalar(
                                    out=hglu[:, c, 0:32], in0=hglu[:, c, 0:32], scalar1=vcol(V_FLAG), scalar2=None, op0=ALU.mult),
                                    reads=[("hglu", c), "vecs"], writes=[("hglu", c)], cost=0.3)
                            yield

                def att_gen(qc):
                    for _ in attention(qc):
                        yield

                def run_gens(gens):
                    gens = list(gens)
                    while gens:
                        for g in list(gens):
                            try:
                                next(g)
                            except StopIteration:
                                gens.remove(g)

                run_gens([prep_gen(0)])
                for h in range(4):
                    P.op("sp", lambda e, h=h: e.dma_start(out=Kaug[h][64:80, :], in_=oneh_d),
                         reads=[("Kaug", 0, 0)], writes=[("Kaug_oh", h)], dma=True)
                if pas == 0:
                    issue_w(0, wsb, which=[0, 3], after=[("Kaug", 0, 0)])
                run_gens([prep_gen(1)])
                run_gens([prep_gen(4, do_gating=False)])
                for wb in range(2, 4):
                    run_gens([prep_gen(wb)])
                if pas == 1:
                    for half in range(2):
                        P.op("pool", lambda e, half=half: e.dma_start(
                            out=woutb[:, 4 * half:4 * half + 4, :],
                            in_=w_out[512 * half:512 * half + 512, :].rearrange("(c p) n -> p c n", p=128)),
                            reads=[("Kaug", 0, 3)], writes=[("woutb", half)], dma=True, xfer=15.0)
                run_gens([gating(0)])
                run_gens([att_gen(0), prep_gen(5)])
                run_gens([att_gen(1), prep_gen(6)])
                run_gens([att_gen(2), prep_gen(7)])
                run_gens([att_gen(3)])

                with nc.Block() as blk:
                    P.emit(blk)
                P.barrier()

            if pas == 0:
                for c_ in (2, 3, 4, 5, 6, 7):
                    mixc[c_] = SB(outer, "mixT%d" % c_, [128, T], BF16)
                wsb1 = SB(w1scope, "wsb1", [128, 8, 768], BF16)
                with ExitStack() as sc:
                    diagW = SB(sc, "diagW", [128, 4 * 31, 128], BF16)
                    y32 = [SB(sc, "y32_%d" % i, [128, 4, 512], F32) for i in range(2)]
                    ybf = [SB(sc, "ybf%d" % i, [128, 512], BF16) for i in range(2)]
                    ysq = [SB(sc, "ysq%d" % i, [128, 512], BF16) for i in range(2)]
                    mean = SB(sc, "mean", [128, 512], F32)
                    msq = SB(sc, "msq", [128, 512], F32)
                    var = SB(sc, "var", [128, 512], F32)
                    rstdc = SB(sc, "rstdc", [128, 512], F32)
                    zc = [SB(sc, "zc%d" % i, [128, 512], F32) for i in range(2)]
                    CR = [ps[0], ps[1], ps[2], ps[3]]
                    segs1 = issue_w(1, wsb1)
                    P.op("dve", lambda e: e.tensor_scalar(out=dww[:], in0=dww[:], scalar1=0.5, scalar2=None, op0=ALU.mult),
                         reads=["dww"], writes=["dww"], cost=0.3)
                    for c in range(4):
                        for i in range(31):
                            if i % 2 == 0:
                                P.op("dve", lambda e, c=c, i=i: e.tensor_scalar(
                                    out=diagW[:, c * 31 + i, :], in0=ident, scalar1=dww[:, c * 31 + i:c * 31 + i + 1],
                                    scalar2=None, op0=ALU.mult),
                                    reads=["cbf", "dww"], writes=[("diagW", c, i)], cost=0.3)
                            else:
                                P.op("act", lambda e, c=c, i=i: e.activation(
                                    out=diagW[:, c * 31 + i, :], in_=ident, func=ACT.Copy, scale=dww[:, c * 31 + i:c * 31 + i + 1]),
                                    reads=["cbf", "dww"], writes=[("diagW", c, i)], cost=0.4)
                    pending = []

                    def ln_block(tb, s1, s2):
                        P.op("act", lambda e, s1=s1: e.activation(out=mean[:], in_=s1[:], func=ACT.Copy, scale=1.0 / 512.0),
                             reads=[pk(s1)], writes=["mean"])
                        P.op("dve", lambda e: e.tensor_tensor(out=msq[:], in0=mean[:], in1=mean[:], op=ALU.mult),
                             reads=["mean"], writes=["msq"], cost=0.7)
                        P.op("dve", lambda e, s2=s2: e.scalar_tensor_tensor(out=var[:], in0=s2[:], scalar=1.0 / 512.0, in1=msq[:],
                                                                             op0=ALU.mult, op1=ALU.subtract),
                             reads=[pk(s2), "msq"], writes=["var"])
                        P.op("act", lambda e: e.activation(out=var[:], in_=var[:], func=ACT.Ln, scale=1.0, bias=vcol(V_E1)),
                             reads=["var", "vecs"], writes=["var"])
                        P.op("act", lambda e: e.activation(out=rstdc[:], in_=var[:], func=ACT.Exp, scale=-0.5),
                             reads=["var"], writes=["rstdc"])
                        for c in range(4):
                            z = zc[c % 2]
                            P.op("pool" if c % 2 == 0 else "dve", lambda e, c=c, z=z, tb=tb: e.tensor_tensor(out=z[:], in0=y32[tb % 2][:, c, :], in1=mean[:], op=ALU.subtract),
                                 reads=[("y32", tb % 2, c), "mean"], writes=[("zc", c % 2)], cost=(1.3 if c % 2 == 0 else 0.7))
                            P.op("dve", lambda e, c=c, z=z: e.tensor_tensor(out=z[:], in0=z[:], in1=rstdc[:], op=ALU.mult),
                                 reads=[("zc", c % 2), "rstdc"], writes=[("zc", c % 2)])
                            P.op("act", lambda e, c=c, z=z, tb=tb: e.activation(
                                out=mixc[4 + c][:, tb * 512:(tb + 1) * 512], in_=z[:], func=ACT.Silu,
                                scale=vcol(V_LNG + c), bias=vcol(V_LNB + c)),
                                reads=[("zc", c % 2), "vecs"], writes=[("mixT", 4 + c, tb)])

                    for tb in range(4):
                        s1 = bank("c2", [ps[4], ps[6]])
                        s2 = bank("c3", [ps[5], ps[7]])
                        for c in range(4):
                            pc = bank("c", CR)

                            def fC(e, pc=pc, c=c, tb=tb):
                                ins = None
                                for i in range(31):
                                    a0 = 32 + tb * 512 - 30 + i
                                    ins = e.matmul(pc[:], lhsT=diagW[:, c * 31 + i, :], rhs=hglu[:, c, a0:a0 + 512],
                                                   start=(i == 0), stop=(i == 30))
                                return ins
                            P.op("pe", fC, reads=[("diagW", c, i) for i in range(31)] + [("hglu", c), "hglu_halo"], writes=[pk(pc)], cost=6.8)
                            for fn in pending:
                                fn()
                            pending = []
                            P.op("dve", lambda e, pc=pc, c=c, tb=tb: e.tensor_scalar(
                                out=y32[tb % 2][:, c, :], in0=pc[:], scalar1=vcol(V_DWB + c), scalar2=None, op0=ALU.add),
                                reads=[pk(pc), "vecs"], writes=[("y32", tb % 2, c)])
                            P.op("act", lambda e, c=c, tb=tb: e.copy(out=ybf[c % 2][:], in_=y32[tb % 2][:, c, :]),
                                 reads=[("y32", tb % 2, c)], writes=[("ybf", c % 2)], cost=0.6)
                            P.op("dve", lambda e, c=c, tb=tb: e.tensor_tensor(out=ysq[c % 2][:], in0=y32[tb % 2][:, c, :], in1=y32[tb % 2][:, c, :], op=ALU.mult),
                                 reads=[("y32", tb % 2, c)], writes=[("ysq", c % 2)])

                            def stats(c=c, s1=s1, s2=s2, tb=tb):
                                P.op("pe", lambda e: e.matmul(s1[:], lhsT=allones, rhs=ybf[c % 2][:], start=(c == 0), stop=(c == 3)),
                                     reads=[("ybf", c % 2), "cbf"], writes=[pk(s1)])
                                P.op("pe", lambda e: e.matmul(s2[:], lhsT=allones, rhs=ysq[c % 2][:], start=(c == 0), stop=(c == 3)),
                                     reads=[("ysq", c % 2), "cbf"], writes=[pk(s2)])
                                if c == 3:
                                    ln_block(tb, s1, s2)
                            pending.append(stats)
                    for fn in pending:
                        fn()
                    with nc.Block() as blk:
                        P.emit(blk)
                    P.barrier()
                hscope.close()
                woutb = outer.enter_context(nc.sbuf_tensor("woutb", [128, 8, D], BF16, side="right"))

        w1scope.close()
        if DEBUG:
            for c in range(8):
                P.op("sp", lambda e, c=c: e.dma_start(out=dbg[:, c * T:(c + 1) * T], in_=mixc[c][:, :]), dma=True)
        with ExitStack() as sc:
            wdb = SB(sc, "wdb", [128, NFF, D], BF16)
            wgu = [SB(sc, "wgu%d" % i, [128, 8, 256], BF16) for i in range(4)]
            actT = SB(sc, "actT", [128, NFF, 512], BF16)
            x1 = [SB(sc, "x1_%d" % i, [128, 4, D], F32) for i in range(2)]
            xt2 = [SB(sc, "xt2_%d" % i, [128, D], F32) for i in range(2)]
            xn2 = [SB(sc, "xn2_%d" % i, [128, D], BF16) for i in range(2)]
            ssq2 = [SB(sc, "ssq2_%d" % i, [128, 1], F32) for i in range(2)]
            rstd2 = [SB(sc, "rstd2_%d" % i, [128, 1], F32) for i in range(2)]
            sg = [SB(sc, "sg%d" % i, [128, 512], F32) for i in range(2)]
            osb = [SB(sc, "osb%d" % i, [128, 512], F32) for i in range(2)]
            A8 = list(ps)


            def issue_wd():
                for q in range(2):
                    P.op("pool", lambda e, q=q: e.dma_start(
                        out=wdb[:, 11 * q:11 * q + 11, :],
                        in_=w_down[1408 * q:1408 * q + 1408, :].rearrange("(c p) n -> p c n", p=128)),
                        reads=[("wgu", 3)], writes=[("wdb", q)], dma=True, xfer=40.0)

            fcount = [0]

            def outnorm_gen(tb):
                xb1 = x1[tb % 2]
                for t in range(4):
                    tt = 4 * tb + t
                    xb = tt % 2
                    P.op("sp", lambda e, tt=tt, xb=xb: e.dma_start(out=xt2[xb][:], in_=xw[T + tt * 128:T + (tt + 1) * 128, :]),
                         writes=[("xt2", xb)], dma=True)
                    for half in range(2):
                        po = bank("a", A8)

                        def fOP(e, po=po, tt=tt, half=half):
                            ins = None
                            for c in range(8):
                                ins = e.matmul(po[:], lhsT=mixc[c][:, tt * 128:(tt + 1) * 128],
                                               rhs=woutb[:, c, 512 * half:512 * half + 512], start=(c == 0), stop=(c == 7))
                            return ins
                        P.op("pe", fOP, reads=[("mixT_tile", tt), ("woutb", 0), ("woutb", 1)], writes=[pk(po)], cost=1.8)
                        yield
                        P.op("dve", lambda e, po=po, xb1=xb1, t=t, half=half, xb=xb: e.tensor_tensor(
                            out=xb1[:, t, 512 * half:512 * half + 512], in0=po[:], in1=xt2[xb][:, 512 * half:512 * half + 512], op=ALU.add),
                            reads=[pk(po), ("xt2", xb)], writes=[("x1", tb % 2, t)])
                    yield
                    P.op("act", lambda e, xb=xb, xb1=xb1, t=t: e.activation(out=xn2[xb][:], in_=xb1[:, t, :], func=ACT.Square, accum_out=ssq2[xb][:]),
                         reads=[("x1", tb % 2, t)], writes=[("xn2", xb), ("ssq2", xb)], cost=1.05)
                    P.op("dve", lambda e, xb=xb: e.tensor_scalar(out=rstd2[xb][:], in0=ssq2[xb][:], scalar1=1.0 / D, scalar2=EPS, op0=ALU.mult, op1=ALU.add),
                         reads=[("ssq2", xb)], writes=[("rstd2", xb)], cost=0.2)
                    P.op("pool", lambda e, xb=xb: e.tensor_tensor(out=rstd2[xb][:], in0=rstd2[xb][:], in1=cneg[:, 1:2], op=ALU.pow),
                         reads=[("rstd2", xb), "cneg1"], writes=[("rstd2", xb)], cost=0.5)
                    yield
                    yield
                    P.op("dve", lambda e, xb=xb, xb1=xb1, t=t: e.tensor_scalar(out=xn2[xb][:], in0=xb1[:, t, :], scalar1=rstd2[xb][:],
                                                                                scalar2=None, op0=ALU.mult),
                         reads=[("x1", tb % 2, t), ("rstd2", xb)], writes=[("xn2", xb)])
                    yield
                    yield
                    for half in range(2):
                        tp = bank("a", A8)

                        def fTr(e, tp=tp, half=half, xb=xb):
                            ins = None
                            for cc in range(4):
                                c = 4 * half + cc
                                ins = e.matmul(tp[:, 128 * cc:128 * cc + 128], lhsT=xn2[xb][:, 128 * c:128 * c + 128],
                                               rhs=ident, start=True, stop=True)
                            return ins
                        P.op("pe", fTr, reads=[("xn2", xb), "cbf"], writes=[pk(tp)], cost=0.5)
                        yield
                        for cc in range(4):
                            c = 4 * half + cc
                            if cc % 2 == 0:
                                P.op("dve", lambda e, tp=tp, cc=cc, c=c, tt=tt: e.tensor_scalar(
                                    out=mixc[c][:, tt * 128:(tt + 1) * 128], in0=tp[:, 128 * cc:128 * cc + 128],
                                    scalar1=vcol(V_G2 + c), scalar2=None, op0=ALU.mult),
                                    reads=[pk(tp), "vecs"], writes=[("mixT_tile", tt)])
                            else:
                                P.op("act", lambda e, tp=tp, cc=cc, c=c, tt=tt: e.activation(
                                    out=mixc[c][:, tt * 128:(tt + 1) * 128], in_=tp[:, 128 * cc:128 * cc + 128],
                                    func=ACT.Copy, scale=vcol(V_G2 + c)),
                                    reads=[pk(tp), "vecs"], writes=[("mixT_tile", tt)])
                    yield

            def gateup_gen(tb):
                for f_ in range(NFF):
                    slot = fcount[0] % 4
                    fcount[0] += 1
                    wt_ = wgu[slot]
                    P.op("pool", lambda e, f_=f_, wt_=wt_: e.dma_start(
                        out=wt_[:, :, :], in_=w_gu[:, f_ * 256:(f_ + 1) * 256].rearrange("(c p) n -> p c n", p=128)),
                        writes=[("wgu", slot)], dma=True, xfer=6.0)
                    if tb == 0 and f_ == 3:
                        issue_wd()
                    pg = bank("a", A8)
                    pu = bank("a", A8)

                    def fGU(e, pg=pg, pu=pu, wt_=wt_, tb=tb):
                        ins = None
                        for c in range(8):
                            ins = e.matmul(pg[:], lhsT=wt_[:, c, 0:128], rhs=mixc[c][:, tb * 512:(tb + 1) * 512],
                                           start=(c == 0), stop=(c == 7))
                        for c in range(8):
                            ins = e.matmul(pu[:], lhsT=wt_[:, c, 128:256], rhs=mixc[c][:, tb * 512:(tb + 1) * 512],
                                           start=(c == 0), stop=(c == 7))
                        return ins
                    tiles = [("mixT_tile", 4 * tb + t) for t in range(4)]
                    P.op("pe", fGU, reads=[("wgu", slot)] + tiles, writes=[pk(pg), pk(pu)], cost=3.6)
                    sgi = sg[f_ % 2]
                    P.op("act", lambda e, pg=pg, sgi=sgi: e.activation(out=sgi[:], in_=pg[:], func=ACT.Silu),
                         reads=[pk(pg)], writes=[("sg", f_ % 2)])
                    P.op("dve", lambda e, pu=pu, sgi=sgi, f_=f_: e.tensor_tensor(out=actT[:, f_, :], in0=pu[:], in1=sgi[:], op=ALU.mult),
                         reads=[pk(pu), ("sg", f_ % 2)], writes=[("actT", f_)])
                    yield

            def down_gen(tb):
                xb1 = x1[tb % 2]
                for t in range(4):
                    tt = 4 * tb + t
                    for half in range(2):
                        pd = bank("a", A8)

                        def fD(e, pd=pd, t=t, half=half):
                            ins = None
                            for f_ in range(NFF):
                                ins = e.matmul(pd[:], lhsT=actT[:, f_, 128 * t:128 * t + 128],
                                               rhs=wdb[:, f_, 512 * half:512 * half + 512], start=(f_ == 0), stop=(f_ == NFF - 1))
                            return ins
                        P.op("pe", fD, reads=[("actT", f_) for f_ in range(NFF)] + [("wdb", 0), ("wdb", 1)], writes=[pk(pd)], cost=4.9)
                        ob = osb[(2 * t + half) % 2]
                        P.op("dve", lambda e, pd=pd, ob=ob, xb1=xb1, t=t, half=half: e.tensor_tensor(
                            out=ob[:], in0=pd[:], in1=xb1[:, t, 512 * half:512 * half + 512], op=ALU.add),
                            reads=[pk(pd), ("x1", tb % 2, t)], writes=[("osb", (2 * t + half) % 2)])
                        P.op("sp", lambda e, ob=ob, tt=tt, half=half: e.dma_start(
                            out=y[tt * 128:(tt + 1) * 128, 512 * half:512 * half + 512], in_=ob[:]),
                            reads=[("osb", (2 * t + half) % 2)], dma=True)
                        yield

            def run_gens2(gens):
                gens = list(gens)
                while gens:
                    for g in list(gens):
                        try:
                            next(g)
                        except StopIteration:
                            gens.remove(g)

            run_gens2([outnorm_gen(0)])
            for tb in range(4):
                gl = [gateup_gen(tb)]
                if tb + 1 < 4:
                    gl.append(outnorm_gen(tb + 1))
                run_gens2(gl)
                run_gens2([down_gen(tb)])
            with nc.Block() as blk:
                P.emit(blk, final_wait=True)
    return nc


_NC_CACHE = {}


def _consts():
    bf = ml_dtypes.bfloat16
    p = np.arange(128)
    ident = np.eye(128, dtype=np.float32)
    tri = np.where(p[:, None] <= p[None, :], 0.0, NEG).astype(np.float32)
    blockones = (p[:, None] // 64 == p[None, :] // 64).astype(np.float32)
    partner = (p // 64) * 64 + (p % 64 + 32) % 64
    perm = (p[:, None] == partner[None, :]).astype(np.float32)
    allones = np.ones((128, 128), np.float32)
    cbf = np.concatenate([ident, tri, blockones, perm, allones], axis=1).astype(bf)
    keyblk = np.arange(W) // 256
    oneh = np.where(keyblk[None, :] == np.arange(16)[:, None], NEG, 0.0).astype(np.float32).astype(bf)
    return cbf, oneh


def _rope_tables(half):
    pos = (np.arange(W, dtype=np.int64) + half * T - T)
    posf = np.maximum(pos, 0).astype(np.float32)
    inv_freq = (np.float32(10000.0) ** (-np.arange(0, 64, 2, dtype=np.float32) / np.float32(64))).astype(np.float32)
    ang = (posf[:, None] * inv_freq[None, :]).astype(np.float32)
    ang = np.concatenate([ang, ang], axis=-1)
    cos = np.cos(ang).astype(np.float32).T
    sin = np.sin(ang).astype(np.float32).T
    sgn = np.where(np.arange(64) < 32, -1.0, 1.0).astype(np.float32)[:, None]
    sin = sin * sgn
    cos2 = np.concatenate([cos, cos], axis=0)
    sin2 = np.concatenate([sin, sin], axis=0)
    return np.ascontiguousarray(cos2), np.ascontiguousarray(sin2)


def _elig(half):
    el = np.zeros((16, 16), np.float32)
    for qt in range(16):
        ob = qt // 2
        for j in range(16):
            if j < 8:
                v = 0.0 if half == 1 else -BIGF
            elif j < 8 + ob:
                v = 0.0
            elif j == 8 + ob:
                v = BIGF
            else:
                v = -BIGF
            el[qt, j] = v
    inel = (el < -1e29).astype(np.float32)
    el_b = np.broadcast_to(el.reshape(1, 256), (128, 256)).copy()
    in_b = np.broadcast_to(inel.reshape(1, 256), (128, 256)).copy()
    return el_b, in_b


def kernel(x, norm1_g, w_in, glu_b, q_norm_g, k_norm_g, dw_w, dw_b, conv_ln_g, conv_ln_b,
           w_out, norm2_g, w_gate, w_up, w_down):
    f32 = np.float32
    x = np.asarray(x, f32)

    def fm(v, n):
        return np.asarray(v, f32).reshape(n, 128).T

    vecs = np.zeros((128, 64), f32)
    vecs[:, V_G1:V_G1 + 8] = fm(norm1_g[0], 8)
    vecs[:, V_G2:V_G2 + 8] = fm(norm2_g[0], 8)
    vecs[:, V_GLU:V_GLU + 8] = fm(glu_b[0], 8)
    vecs[:, V_GQ] = np.tile(np.asarray(q_norm_g[0], f32), 2)
    vecs[:, V_GK] = np.tile(np.asarray(k_norm_g[0], f32), 2)
    vecs[:, V_DWB:V_DWB + 4] = fm(dw_b[0], 4)
    vecs[:, V_LNG:V_LNG + 4] = fm(conv_ln_g[0], 4)
    vecs[:, V_LNB:V_LNB + 4] = fm(conv_ln_b[0], 4)
    dww = np.asarray(dw_w[0], f32).reshape(31, 4, 128).transpose(2, 1, 0).reshape(128, 4 * 31)
    dww = np.ascontiguousarray(dww)
    cbf, oneh = _consts()

    shared = {
        "w_in": np.ascontiguousarray(np.asarray(w_in[0], f32)),
        "w_out": np.ascontiguousarray(np.asarray(w_out[0], f32)),
        "w_gu": np.ascontiguousarray(np.concatenate(
            [np.asarray(w_gate[0], f32).reshape(D, NFF, 128), np.asarray(w_up[0], f32).reshape(D, NFF, 128)], axis=2).reshape(D, NFF * 256)),
        "w_down": np.ascontiguousarray(np.asarray(w_down[0], f32)),
        "dww": dww, "cbf": cbf, "oneh": oneh,
    }
    tabs = [_rope_tables(h) for h in range(2)]
    eligs = [_elig(h) for h in range(2)]
    in_maps = []
    for core in range(8):
        b, half = core // 2, core % 2
        xw = np.zeros((W, D), f32)
        if half == 1:
            xw[:] = x[b]
        else:
            xw[T:] = x[b, :T]
        v = vecs.copy()
        v[:, V_FLAG] = float(half)
        v[:, V_E1] = EPS
        v[:, V_E64] = 64.0 * EPS
        m = dict(shared)
        m.update({"xw": xw, "vecs": v, "cos_t": tabs[half][0], "sin_t": tabs[half][1],
                  "elig": eligs[half][0], "inel": eligs[half][1]})
        in_maps.append(m)

    if "nc" not in _NC_CACHE:
        _NC_CACHE["nc"] = build_program()
    nc = _NC_CACHE["nc"]
    res = run_bass_kernel_spmd(nc, in_maps, core_ids=list(range(8)))
    out = np.zeros((B, S, D), f32)
    for core in range(8):
        b, half = core // 2, core % 2
        out[b, half * T:(half + 1) * T] = res.results[core]["y"]
    if DEBUG:
        kernel.dbg = [res.results[c]["dbg"] for c in range(8)]
    return out
```

```python
import numpy as np
import ml_dtypes
from contextlib import ExitStack

import concourse.bass as bass
import concourse.mybir as mybir
from concourse.bass_utils import run_bass_kernel_spmd

F32 = mybir.dt.float32
BF16 = mybir.dt.bfloat16
ALU = mybir.AluOpType
ACT = mybir.ActivationFunctionType
AX = mybir.AxisListType

D = 1024
S = 4096
B = 4
T = 2048
W = 4096
DFF = 2816
NFF = DFF // 128
EPS = 1e-6
NEG = -32768.0
BIGF = 1.0e30
DEBUG = False

NDSEM = 8
ATTACH_WAIT = True


class _PEProxy:
    def __init__(self, e):
        self.e = e
        self.first = None

    def matmul(self, *a, **k):
        ins = self.e.matmul(*a, **k)
        if self.first is None:
            self.first = ins
        return ins


class Op:
    __slots__ = ("eng", "fn", "isdma", "deps", "raw", "cost", "xfer", "uid", "pos", "sem", "val", "waits", "bl", "nsucc", "succ", "own")


DEF_COST = {"pe": 0.3, "act": 0.65, "dve": 0.7, "pool": 1.3, "sp": 0.1}
HOP = 0.2


class Prog:
    COMPUTE = ("pe", "act", "dve", "pool")
    ENGS = ("pe", "act", "dve", "pool", "sp")

    def __init__(self, nc):
        self.nc = nc
        self.cur = []
        self.count = {e: 0 for e in self.COMPUTE}
        self.dcount = {"sp": 0, "pool": 0}
        self.last_writer = {}
        self.readers = {}
        self.waited = {}
        self.sems = {}
        self.dma_final = {}
        self.barrier_snap = None
        self.uid = 0

    def alloc_sems(self, stack):
        nc = self.nc
        for e in self.COMPUTE:
            self.sems[e] = stack.enter_context(nc.semaphore("c_" + e))
        for q in ("sp", "pool"):
            for i in range(NDSEM):
                self.sems[(q, i)] = stack.enter_context(nc.semaphore("d_%s%d" % (q, i)))
        for i in range(4):
            self.sems[("x", i)] = stack.enter_context(nc.semaphore("d_x%d" % i))

    def barrier(self):
        assert not self.cur
        snap = []
        for e in self.COMPUTE:
            if self.count[e] > 0:
                snap.append((e, self.count[e]))
        for k, v in self.dma_final.items():
            snap.append((k, v))
        self.barrier_snap = snap
        self.last_writer = {}
        self.readers = {}

    def op(self, eng, fn, reads=(), writes=(), dma=False, cost=None, xfer=3.0, own=None):
        o = Op()
        o.eng = eng
        o.fn = fn
        o.isdma = dma
        o.cost = (0.1 if dma else DEF_COST[eng]) if cost is None else cost
        o.xfer = xfer if dma else 0.0
        o.own = own
        o.uid = self.uid
        self.uid += 1
        deps = set()
        raw = set()
        for k in reads:
            w = self.last_writer.get(k)
            if w is not None:
                deps.add(w)
                raw.add(w)
        for k in writes:
            w = self.last_writer.get(k)
            if w is not None:
                deps.add(w)
            deps.update(self.readers.get(k, ()))
        deps.discard(o)
        o.deps = deps
        o.raw = raw
        for k in reads:
            self.readers.setdefault(k, []).append(o)
        for k in writes:
            self.last_writer[k] = o
            self.readers[k] = []
        self.cur.append(o)
        return o

    def _schedule(self, ops):
        import heapq
        inphase = set(id(o) for o in ops)
        for o in ops:
            o.deps = [d for d in o.deps if id(d) in inphase]
            o.succ = []
        for o in ops:
            for d in o.deps:
                d.succ.append(o)
        for o in reversed(ops):
            m = 0.0
            for s_ in o.succ:
                if s_.bl > m:
                    m = s_.bl
            o.bl = o.cost + o.xfer + m
        ndep = {id(o): len(o.deps) for o in ops}
        finish = {}
        ready = {e: [] for e in self.ENGS}
        for o in ops:
            if ndep[id(o)] == 0:
                heapq.heappush(ready[o.eng], (-o.bl, o.uid, o))
        free = {e: 0.0 for e in self.ENGS}
        order = {e: [] for e in self.ENGS}
        nleft = len(ops)

        def dready(o):
            t = 0.0
            for d in o.deps:
                f = finish[id(d)] + (HOP if (d.eng != o.eng or d.isdma) else 0.0)
                if f > t:
                    t = f
            return t
        while nleft:
            best = None
            for e in self.ENGS:
                h = ready[e]
                if not h:
                    continue
                cands = heapq.nsmallest(6, h)
                pick = None
                pick_t = None
                for c in cands:
                    t = max(free[e], dready(c[2]))
                    if t <= free[e] + 1e-9:
                        pick, pick_t = c, t
                        break
                    if pick is None or t < pick_t:
                        pick, pick_t = c, t
                if best is None or pick_t < best[0]:
                    best = (pick_t, e, pick)
            t0, e, c = best
            ready[e].remove(c)
            heapq.heapify(ready[e])
            o = c[2]
            free[e] = t0 + o.cost
            finish[id(o)] = t0 + o.cost + o.xfer
            order[e].append(o)
            nleft -= 1
            for s_ in o.succ:
                ndep[id(s_)] -= 1
                if ndep[id(s_)] == 0:
                    heapq.heappush(ready[s_.eng], (-s_.bl, s_.uid, s_))
        return order

    def emit(self, block, final_wait=False):
        ops = self.cur
        self.cur = []
        order = self._schedule(ops)
        sems = self.sems
        for e in self.ENGS:
            for i, o in enumerate(order[e]):
                o.pos = i
                if o.isdma and o.own is not None:
                    o.sem = ("x", o.own)
                    o.val = 16
                    self.dma_final[o.sem] = o.val
                elif o.isdma:
                    n = self.dcount[e]
                    self.dcount[e] += 1
                    o.sem = (e, n % NDSEM)
                    o.val = 16 * (n // NDSEM + 1)
                    self.dma_final[o.sem] = o.val
                else:
                    self.count[e] += 1
                    o.sem = e
                    o.val = self.count[e]
        for e in self.ENGS:
            first = True
            for o in order[e]:
                o.waits = []

                def addw(sk, v, o=o, e=e):
                    k = (e, sk)
                    if self.waited.get(k, 0) >= v:
                        return
                    self.waited[k] = v
                    o.waits.append((sk, v))
                if first and self.barrier_snap:
                    for (sk, v) in self.barrier_snap:
                        if sk == e and not o.isdma:
                            continue
                        addw(sk, v)
                first = False
                for d in o.deps:
                    if (not d.isdma) and d.eng == e and not o.isdma:
                        if e == "pe":
                            continue
                        if o.pos - d.pos > 2:
                            if self.waited.get((e, e), 0) >= d.val:
                                continue
                            v = d.val
                            for back in range(3, 8):
                                if o.pos - back < 0:
                                    break
                                q = order[e][o.pos - back]
                                if not q.isdma:
                                    v = max(v, q.val)
                                    break
                            addw(e, v)
                            continue
                    addw(d.sem, d.val)
                if o.isdma and o.val > 16:
                    addw(o.sem, o.val - 16)
                if len(o.waits) > 1:
                    mx = {}
                    for (sk, v) in o.waits:
                        if mx.get(sk, 0) < v:
                            mx[sk] = v
                    o.waits = list(mx.items())
        self.barrier_snap = None
        final = list(self.dma_final.items()) if final_wait else []

        def run(engname, e):
            for o in order[engname]:
                ws = list(o.waits)
                att = None
                if ATTACH_WAIT and ws and not o.isdma:
                    att = ws.pop()
                for (sk, v) in ws:
                    e.wait_ge(sems[sk], v)
                if engname == "pe":
                    px = _PEProxy(e)
                    ins = o.fn(px)
                    first = px.first
                else:
                    ins = o.fn(e)
                    first = ins
                if att is not None:
                    first._wait_ge(sems[att[0]], att[1])
                if o.isdma:
                    ins.then_inc(sems[o.sem], 16)
                else:
                    ins.then_inc(sems[o.sem], 1)
            if engname == "sp":
                for (sk, v) in final:
                    e.wait_ge(sems[sk], v)

        @block.tensor
        def _(e):
            run("pe", e)

        @block.scalar
        def _(e):
            run("act", e)

        @block.vector
        def _(e):
            run("dve", e)

        @block.gpsimd
        def _(e):
            run("pool", e)

        @block.sync
        def _(e):
            run("sp", e)


NPTS = [8, 5]
WIDE_RING = True
XN_ACT = False
NHT = [2, 2]
NSCR = [2, 2]
NCS = [2, 2]
NXT = [4, 4]
V_G1, V_G2, V_GLU, V_GQ, V_GK, V_DWB, V_LNG, V_LNB, V_FLAG, V_E1, V_E64 = 0, 8, 16, 24, 25, 26, 30, 34, 38, 39, 40


def build_program():
    nc = bass.Bass("TRN2", target_bir_lowering=False)

    def din(name, shape, dt=F32):
        return nc.dram_tensor(name, list(shape), dt, kind="ExternalInput").ap()

    xw = din("xw", [W, D])
    w_in = din("w_in", [D, 2560])
    w_out = din("w_out", [D, D])
    w_gu = din("w_gu", [D, NFF * 256])
    w_down = din("w_down", [DFF, D])
    vecs_d = din("vecs", [128, 64])
    dww_d = din("dww", [128, 4 * 31])
    cos_d = din("cos_t", [128, W])
    sin_d = din("sin_t", [128, W])
    elig_d = din("elig", [128, 256])
    inel_d = din("inel", [128, 256])
    cbf_d = din("cbf", [128, 5 * 128], BF16)
    oneh_d = din("oneh", [16, W], BF16)
    y = nc.dram_tensor("y", [T, D], F32, kind="ExternalOutput").ap()
    dbg = None
    if DEBUG:
        dbg = nc.dram_tensor("dbg", [128, 8 * T], BF16, kind="ExternalOutput").ap()

    with ExitStack() as outer:
        P = Prog(nc)
        P.alloc_sems(outer)

        def SB(st, name, shape, dt):
            return st.enter_context(nc.sbuf_tensor(name, list(shape), dt))

        ps = [outer.enter_context(nc.psum_tensor("psb%d" % i, [128, 512], F32)) for i in range(8)]
        psname = {id(p): i for i, p in enumerate(ps)}
        ring_state = {}

        def bank(group, banks):
            i = ring_state.get(group, 0)
            ring_state[group] = i + 1
            return banks[i % len(banks)]

        def pk(p):
            return ("ps", psname[id(p)])

        mixc = [None] * 8
        for c_ in (0, 1):
            mixc[c_] = SB(outer, "mixT%d" % c_, [128, T], BF16)
        vecs = SB(outer, "vecs_sb", [128, 64], F32)
        dww = SB(outer, "dww_sb", [128, 4 * 31], F32)
        cbf = SB(outer, "cbf_sb", [128, 5 * 128], BF16)
        elig = SB(outer, "elig_sb", [128, 256], F32)
        inel = SB(outer, "inel_sb", [128, 256], F32)
        ident = cbf[:, 0:128]
        tri = cbf[:, 128:256]
        blockones = cbf[:, 256:384]
        perm = cbf[:, 384:512]
        allones = cbf[:, 512:640]

        P.op("sp", lambda e: e.dma_start(out=vecs[:], in_=vecs_d), writes=["vecs"], dma=True)
        P.op("sp", lambda e: e.dma_start(out=dww[:], in_=dww_d), writes=["dww"], dma=True)
        P.op("sp", lambda e: e.dma_start(out=cbf[:], in_=cbf_d), writes=["cbf"], dma=True)
        P.op("sp", lambda e: e.dma_start(out=elig[:], in_=elig_d), writes=["elig"], dma=True)
        P.op("sp", lambda e: e.dma_start(out=inel[:], in_=inel_d), writes=["inel"], dma=True)

        def vcol(c):
            return vecs[:, c:c + 1]

        V_HBG = 41
        P.op("dve", lambda e: e.tensor_scalar(out=vecs[:, V_HBG:V_HBG + 4], in0=vecs[:, V_GLU + 4:V_GLU + 8], scalar1=0.5, scalar2=None, op0=ALU.mult),
             reads=["vecs"], writes=["vecs"])
        rstd_all = SB(outer, "rstd_all", [128, 32], F32)
        cneg = SB(outer, "cneg", [128, 2], F32)
        P.op("pool", lambda e: e.memset(cneg[:, 0:1], -1.0), writes=["cneg0"])
        P.op("pool", lambda e: e.memset(cneg[:, 1:2], -0.5), writes=["cneg1"])

        def cbc(k, lo, hi, n):
            return bass.AP(cneg, lo * 2 + k, [[2, hi - lo], [0, n]])

        G3 = [ps[5], ps[6], ps[7]]
        SR = [ps[0], ps[1], ps[2]]
        OR = [ps[3], ps[4]]

        def issue_w(pas, wsb, which=None, after=()):
            wsrc = [(256 * pas, 256), (512 + 256 * pas, 256), (1024 + 256 * pas, 256)]
            if pas == 0:
                wsrc.append((1536, 1024))
            segs = []
            off = 0
            offs = []
            for (c0, n) in wsrc:
                segs.append((off, n))
                offs.append(off)
                off += n
            if which is None:
                which = [1, 2, 0, 3] if pas == 0 else [1, 2, 0]
            for si in which:
                (c0, n) = wsrc[si]
                off = offs[si]
                for hk in range(2):
                    def f(e, c0=c0, n=n, hk=hk, off=off):
                        return e.dma_start(out=wsb[:, 4 * hk:4 * hk + 4, off:off + n],
                                           in_=w_in[512 * hk:512 * hk + 512, c0:c0 + n].rearrange("(c p) n -> p c n", p=128))
                    P.op("pool", f, reads=list(after), writes=[("wsb", pas, off, kc) for kc in range(4 * hk, 4 * hk + 4)], dma=True,
                         xfer=(6.0 if n == 256 else 20.0))
            return segs

        hscope = ExitStack()
        w1scope = ExitStack()
        hglu = hscope.enter_context(nc.sbuf_tensor("hglu", [128, 4, 32 + T], BF16, side="right"))

        for pas in range(2):
            with ExitStack() as sc:
                Kaug = [SB(sc, "Kaug%d_%d" % (pas, h), [128, W], BF16) for h in range(4)]
                Vaug = SB(sc, "Vaug%d" % pas, [128, 32, 384], BF16)
                Qaug = [[SB(sc, "Qaug%d_%d_%d" % (pas, b_, h), [128, 512], BF16) for h in range(4)] for b_ in range(2)]
                if pas == 0:
                    wsb = SB(sc, "wsb0", [128, 8, 1792], BF16)
                else:
                    wsb = wsb1
                xts = [SB(sc, "xt%d_%d" % (pas, i), [128, D], F32) for i in range(NXT[pas])]
                xn4 = SB(sc, "xn4_%d" % pas, [128, 4, D], BF16)
                ssq = [SB(sc, "ssq%d_%d" % (pas, i), [128, 1], F32) for i in range(NXT[pas])]
                rstd1 = [SB(sc, "rstd%d_%d" % (pas, i), [128, 1], F32) for i in range(NXT[pas])]
                hTs = [SB(sc, "hT%d_%d" % (pas, i), [128, 8, 512], BF16) for i in range(NHT[pas])]
                cosb = [SB(sc, "cos%d_%d" % (pas, i), [128, 512], F32) for i in range(NCS[pas])]
                sinb = [SB(sc, "sin%d_%d" % (pas, i), [128, 512], F32) for i in range(NCS[pas])]
                SCR = []
                for i_ in range(NSCR[pas]):
                    SCR.append(dict(
                        sq=SB(sc, "sq%d_%d" % (pas, i_), [128, 512], BF16),
                        rs=SB(sc, "rs%d_%d" % (pas, i_), [128, 512], F32),
                        rc=SB(sc, "rc%d_%d" % (pas, i_), [128, 512], F32),
                        qnb=SB(sc, "qnb%d_%d" % (pas, i_), [128, 512], BF16),
                        t1=SB(sc, "t1_%d_%d" % (pas, i_), [128, 512], F32),
                        t2=SB(sc, "t2_%d_%d" % (pas, i_), [128, 512], F32),
                        ksum=SB(sc, "ksum%d_%d" % (pas, i_), [128, 4], F32)))
                scr_i = [0]
                kmean = [SB(sc, "kmean%d_%d" % (pas, h), [128, 16], BF16) for h in range(4)]
                pT = [SB(sc, "pT%d_%d" % (pas, i), [128, 512], BF16) for i in range(NPTS[pas])]
                NPT = len(pT)
                rd = [SB(sc, "rd%d_%d" % (pas, i), [128, 512], F32) for i in range(2)]
                Gm = SB(sc, "Gm%d" % pas, [128, 64], F32)
                top8 = SB(sc, "top8_%d" % pas, [128, 32], F32)
                Bt = SB(sc, "Bt%d" % pas, [128, 4, 128], BF16)
                sig = SB(sc, "sig%d" % pas, [128, 512], F32) if pas == 0 else None

                segs = issue_w(pas, wsb, which=[1, 2]) if pas == 0 else segs1
                WQ, WK, WV, WU = 0, 256, 512, 768
                hb_ = [0]
                gr_ = ["g", G3]
                seq_ = [0]
                cbk_ = [0]

                def wread(off_, n):
                    return [("wsb", pas, o2, kc) for (o2, nn) in segs if off_ < o2 + nn and off_ + n > o2 for kc in range(8)]

                for q4 in range(4):
                    P.op("pool", lambda e, q4=q4: e.memset(
                        Vaug[:, 8 * q4:8 * q4 + 8, :].rearrange("p k (a b) -> p k a b", a=2)[:, :, :, 64:128], 1.0),
                        writes=[("Vaug_ones", q4)], cost=1.0)
                P.op("pool", lambda e: e.memset(Bt[:, :, :], 0.0), writes=["Bt"])
                for h in range(4):
                    P.op("pool", lambda e, h=h: e.memset(kmean[h][:], 0.0), writes=[("kmean", h)])
                if pas == 0:
                    P.op("pool", lambda e: e.memset(hglu[:, :, 0:32], 0.0), writes=["hglu_halo"])

                def qk_post(pb, is_q, wb, pair):
                    cb = cbk_[0]
                    hA, hB = 2 * pair, 2 * pair + 1
                    gcol = V_GQ if is_q else V_GK
                    si = scr_i[0] % len(SCR)
                    scr_i[0] += 1
                    S_ = SCR[si]
                    sq, rs, rc, qnb, t1, t2, ksum = S_["sq"], S_["rs"], S_["rc"], S_["qnb"], S_["t1"], S_["t2"], S_["ksum"]
                    sd = rs

                    def Y(k):
                        for _ in range(k):
                            yield
                    P.op("act", lambda e: e.activation(out=sq[:], in_=pb[:], func=ACT.Square),
                         reads=[pk(pb)], writes=[("sq", si)], cost=0.6)
                    yield from Y(2)
                    pb2 = bank(gr_[0], gr_[1])
                    P.op("pe", lambda e: e.matmul(pb2[:], lhsT=blockones, rhs=sq[:], start=True, stop=True),
                         reads=[("sq", si), "cbf"], writes=[pk(pb2)], cost=0.25)
                    yield from Y(2)
                    if is_q:
                        P.op("act", lambda e: e.activation(out=sd[:], in_=pb2[:], func=ACT.Ln, scale=1.0, bias=vcol(V_E64)),
                             reads=[pk(pb2), "vecs"], writes=[("rs", si)])
                    else:
                        P.op("act", lambda e: e.activation(out=sd[:], in_=pb2[:], func=ACT.Ln, scale=1.0 / 64.0, bias=vcol(V_E1)),
                             reads=[pk(pb2), "vecs"], writes=[("rs", si)])
                    P.op("act", lambda e: e.activation(out=rs[:], in_=sd[:], func=ACT.Exp, scale=-0.5),
                         reads=[("rs", si)], writes=[("rs", si)])
                    yield from Y(3)
                    P.op("dve", lambda e: e.scalar_tensor_tensor(out=qnb[:], in0=pb[:], scalar=vcol(gcol), in1=rs[:],
                                                                   op0=ALU.mult, op1=ALU.mult),
                         reads=[pk(pb), ("rs", si), "vecs"], writes=[("qnb", si)])
                    P.op("pool", lambda e: e.tensor_tensor(out=rc[:], in0=rs[:], in1=cosb[cb][:], op=ALU.mult),
                         reads=[("rs", si), ("cos", cb)], writes=[("rc", si)], cost=1.3)
                    yield from Y(3)
                    pb3 = bank(gr_[0], gr_[1])
                    P.op("pe", lambda e: e.matmul(pb3[:], lhsT=perm, rhs=qnb[:], start=True, stop=True),
                         reads=[("qnb", si), "cbf"], writes=[pk(pb3)], cost=0.25)
                    yield from Y(2)
                    P.op("dve", lambda e: e.scalar_tensor_tensor(out=t1[:], in0=pb[:], scalar=vcol(gcol), in1=rc[:],
                                                                   op0=ALU.mult, op1=ALU.mult),
                         reads=[pk(pb), ("rc", si), "vecs"], writes=[("t1", si)])
                    P.op("dve", lambda e: e.tensor_tensor(out=t2[:], in0=pb3[:], in1=sinb[cb][:], op=ALU.mult),
                         reads=[pk(pb3), ("sin", cb)], writes=[("t2", si)])
                    yield from Y(3)
                    if is_q:
                        qb = wb % 2
                        P.op("pool", lambda e: e.tensor_tensor(out=Qaug[qb][hA][0:64, :], in0=t1[0:64, :], in1=t2[0:64, :], op=ALU.add),
                             reads=[("t1", si), ("t2", si)], writes=[("Qaug", qb, hA)])
                        P.op("pool", lambda e: e.tensor_tensor(out=Qaug[qb][hB][0:64, :], in0=t1[64:128, :], in1=t2[64:128, :], op=ALU.add),
                             reads=[("t1", si), ("t2", si)], writes=[("Qaug", qb, hB)])
                    else:
                        c0 = wb * 512
                        P.op("pool", lambda e: e.tensor_tensor(out=Kaug[hA][0:64, c0:c0 + 512], in0=t1[0:64, :], in1=t2[0:64, :], op=ALU.add),
                             reads=[("t1", si), ("t2", si)], writes=[("Kaug", hA, wb)])
                        P.op("pool", lambda e: e.tensor_tensor(out=Kaug[hB][0:64, c0:c0 + 512], in0=t1[64:128, :], in1=t2[64:128, :], op=ALU.add),
                             reads=[("t1", si), ("t2", si)], writes=[("Kaug", hB, wb)])
                        P.op("dve", lambda e: e.tensor_reduce(out=ksum[:, 0:2], in_=t1[:].rearrange("p (b k) -> p b k", b=2),
                                                               axis=AX.X, op=ALU.add),
                             reads=[("t1", si)], writes=[("ksum1", si)])
                        P.op("dve", lambda e: e.tensor_reduce(out=ksum[:, 2:4], in_=t2[:].rearrange("p (b k) -> p b k", b=2),
                                                               axis=AX.X, op=ALU.add),
                             reads=[("t2", si)], writes=[("ksum2", si)])
                        yield from Y(3)
                        P.op("dve", lambda e: e.tensor_tensor(out=kmean[hA][0:64, 2 * wb:2 * wb + 2], in0=ksum[0:64, 0:2], in1=ksum[0:64, 2:4], op=ALU.add),
                             reads=[("ksum1", si), ("ksum2", si)], writes=[("kmean", hA)], cost=0.2)
                        P.op("dve", lambda e: e.tensor_tensor(out=kmean[hB][0:64, 2 * wb:2 * wb + 2], in0=ksum[64:128, 0:2], in1=ksum[64:128, 2:4], op=ALU.add),
                             reads=[("ksum1", si), ("ksum2", si)], writes=[("kmean", hB)], cost=0.2)
                    yield

                def proj_fm(off_, ntok, tok0):
                    pb = bank(gr_[0], gr_[1])
                    hbi = hb_[0]
                    hT = hTs[hbi]

                    def f(e):
                        ins = None
                        for kc in range(8):
                            ins = e.matmul(pb[:, 0:ntok], lhsT=wsb[:, kc, off_:off_ + 128],
                                           rhs=hT[:, kc, tok0:tok0 + ntok], start=(kc == 0), stop=(kc == 7))
                        return ins
                    P.op("pe", f, reads=wread(off_, 128) + [("hT", hbi, c) for c in range(8)], writes=[pk(pb)], cost=(1.8 if ntok == 512 else 0.6))
                    return pb

                def gating(qc):
                    qb = qc % 2
                    if qc == 0 and WIDE_RING:
                        gr_[0], gr_[1] = "g8", list(ps)
                    elif qc == 0:
                        gr_[0], gr_[1] = "g", G3
                    for t in range(4):
                        qt = 4 * qc + t
                        gp = bank(gr_[0], gr_[1])

                        def fG(e, gp=gp, t=t):
                            ins = None
                            for h in range(4):
                                ins = e.matmul(gp[:, 16 * h:16 * h + 16], lhsT=Qaug[qb][h][0:64, 128 * t:128 * t + 128],
                                               rhs=kmean[h][0:64, :], start=True, stop=True)
                            return ins
                        P.op("pe", fG, reads=[("Qaug", qb, h) for h in range(4)] + [("kmean", h) for h in range(4)],
                             writes=[pk(gp)], cost=0.25)
                        yield
                        yield
                        for h in range(4):
                            P.op("dve", lambda e, gp=gp, h=h, qt=qt: e.tensor_tensor(
                                out=Gm[:, 16 * h:16 * h + 16], in0=gp[:, 16 * h:16 * h + 16],
                                in1=elig[:, 16 * qt:16 * qt + 16], op=ALU.add),
                                reads=[pk(gp), "elig"], writes=[("Gm", h)], cost=0.25)
                            P.op("dve", lambda e, h=h: e.max(out=top8[:, 8 * h:8 * h + 8], in_=Gm[:, 16 * h:16 * h + 16]),
                                 reads=[("Gm", h)], writes=[("top8", h)], cost=0.3)
                            P.op("dve", lambda e, h=h, qt=qt, t=t: e.scalar_tensor_tensor(
                                out=Bt[:, t, 64 + 16 * h:64 + 16 * h + 16], in0=Gm[:, 16 * h:16 * h + 16],
                                scalar=top8[:, 8 * h + 3:8 * h + 4], in1=inel[:, 16 * qt:16 * qt + 16],
                                op0=ALU.is_lt, op1=ALU.add),
                                reads=[("Gm", h), ("top8", h), "inel", "Bt"], writes=[("Bt", t, h)], cost=0.3)
                        yield
                    for h in range(4):
                        bt = bank(gr_[0], gr_[1])

                        def fT(e, h=h, bt=bt):
                            ins = None
                            for t in range(4):
                                ins = e.matmul(bt[0:80, 128 * t:128 * t + 128], lhsT=Bt[:, t, 16 * h:16 * h + 80],
                                               rhs=ident, start=True, stop=True)
                            return ins
                        P.op("pe", fT, reads=[("Bt", t, hh) for t in range(4) for hh in range(4)] + ["Bt", "cbf"],
                             writes=[pk(bt)], cost=0.45)
                        yield
                        yield
                        P.op("dve", lambda e, h=h, bt=bt: e.tensor_copy(out=Qaug[qb][h][64:80, :], in_=bt[64:80, :]),
                             reads=[pk(bt)], writes=[("Qbias", qb, h)], cost=0.45)
                        yield

                def attention(qc):
                    qb = qc % 2
                    nfull = 16 + 4 * qc
                    for h in range(4):
                        pair = h // 2
                        isB = h % 2
                        po = bank("o", OR)
                        steps = []
                        for kt in range(nfull + 4):
                            i = kt - nfull
                            steps.append((kt, 0 if i < 0 else 128 * i, i))
                        n = len(steps)
                        LAG = 2
                        sbank = {}
                        vlo = pair * 192 + (64 if isB else 0)
                        for j in range(n + LAG):
                            if j < n:
                                kt, q0, i = steps[j]
                                sb_ = bank("s", SR)
                                sbank[j] = sb_

                                def fS(e, kt=kt, q0=q0, i=i, sb_=sb_, h=h):
                                    ins = e.matmul(sb_[:, q0:512], lhsT=Kaug[h][0:80, kt * 128:(kt + 1) * 128],
                                                   rhs=Qaug[qb][h][0:80, q0:512], start=True, stop=(i < 0))
                                    if i >= 0:
                                        ins = e.matmul(sb_[:, q0:q0 + 128], lhsT=ident, rhs=tri, start=False, stop=True)
                                    return ins
                                P.op("pe", fS, reads=[("Kaug", h, kt // 4), ("Kaug_oh", h), ("Qaug", qb, h), ("Qbias", qb, h), "cbf"],
                                     writes=[pk(sb_)], cost=(0.24 if i < 0 else 0.32))
                            if j >= LAG:
                                jj = j - LAG
                                kt, q0, i = steps[jj]
                                sb_ = sbank.pop(jj)
                                pt = pT[jj % NPT]
                                P.op("act", lambda e, sb_=sb_, pt=pt, q0=q0: e.activation(
                                    out=pt[:, q0:512], in_=sb_[:, q0:512], func=ACT.Exp),
                                    reads=[pk(sb_)], writes=[("pT", jj % NPT)], cost=0.5)

                                def fO(e, kt=kt, q0=q0, pt=pt, jj=jj, po=po, vlo=vlo, n=n):
                                    return e.matmul(po[:, q0:512], lhsT=Vaug[:, kt, vlo:vlo + 128], rhs=pt[:, q0:512],
                                                    start=(jj == 0), stop=(jj == n - 1))
                                P.op("pe", fO, reads=[("pT", jj % NPT), ("Vaug", kt), ("Vaug_ones", kt // 8)], writes=[pk(po)], cost=0.23)
                            yield
                        r_ = rd[h % 2]
                        cch = 2 * pas + pair
                        c0 = qc * 512
                        lo, hi = (0, 64) if not isB else (64, 128)
                        dl, dh = (64, 128) if not isB else (0, 64)
                        P.op("dve", lambda e, r_=r_, po=po, dl=dl, dh=dh: e.reciprocal(out=r_[dl:dh, :], in_=po[dl:dh, :]),
                             reads=[pk(po)], writes=[("rd", h % 2)], cost=3.4)
                        P.op("dve", lambda e, r_=r_, po=po, cch=cch, c0=c0, lo=lo, hi=hi, dl=dl, dh=dh: e.tensor_tensor(
                            out=mixc[cch][lo:hi, c0:c0 + 512], in0=po[lo:hi, :], in1=r_[dl:dh, :], op=ALU.mult),
                            reads=[pk(po), ("rd", h % 2)], writes=[("mixT", cch, qc, isB)])

                def prep_gen(wb, do_gating=True):
                    own = wb >= 4
                    sq_ = seq_[0]
                    seq_[0] += 1
                    if wb <= 4 and WIDE_RING:
                        gr_[0], gr_[1] = "g8", list(ps)
                    else:
                        gr_[0], gr_[1] = "g", G3
                    cb = sq_ % len(cosb)
                    cbk_[0] = cb
                    hT = hTs[sq_ % len(hTs)]
                    hb_[0] = sq_ % len(hTs)
                    P.op("sp", lambda e, wb=wb, cb=cb: e.dma_start(out=cosb[cb][:], in_=cos_d[:, wb * 512:(wb + 1) * 512]),
                         writes=[("cos", cb)], dma=True)
                    P.op("sp", lambda e, wb=wb, cb=cb: e.dma_start(out=sinb[cb][:], in_=sin_d[:, wb * 512:(wb + 1) * 512]),
                         writes=[("sin", cb)], dma=True)
                    for t in range(4):
                        wt = 4 * wb + t
                        xb = wt % len(xts)
                        P.op("sp", lambda e, wt=wt, xb=xb: e.dma_start(out=xts[xb][:], in_=xw[wt * 128:(wt + 1) * 128, :]),
                             writes=[("xt", xb)], dma=True)
                        if pas == 0:
                            P.op("act", lambda e, xb=xb, t=t: e.activation(out=xn4[:, t, :], in_=xts[xb][:], func=ACT.Square, accum_out=ssq[xb][:]),
                                 reads=[("xt", xb)], writes=[("xn", t), ("ssq", xb)], cost=1.05)
                            P.op("dve", lambda e, xb=xb: e.tensor_scalar(out=rstd1[xb][:], in0=ssq[xb][:], scalar1=1.0 / D, scalar2=EPS, op0=ALU.mult, op1=ALU.add),
                                 reads=[("ssq", xb)], writes=[("rstd", xb)], cost=0.2)
                            P.op("pool", lambda e, xb=xb, wt=wt: e.tensor_tensor(out=rstd_all[:, wt:wt + 1], in0=rstd1[xb][:], in1=cneg[:, 1:2], op=ALU.pow),
                                 reads=[("rstd", xb), "cneg1"], writes=[("rstdall", wt)], cost=0.5)
                        if pas == 1 and wb <= 4 and XN_ACT:
                            P.op("act", lambda e, xb=xb, t=t, wt=wt: e.activation(out=xn4[:, t, :], in_=xts[xb][:], func=ACT.Copy, scale=rstd_all[:, wt:wt + 1]),
                                 reads=[("xt", xb), ("rstdall", wt)], writes=[("xn", t)], cost=1.25)
                        else:
                            P.op("dve", lambda e, xb=xb, t=t, wt=wt: e.tensor_scalar(out=xn4[:, t, :], in0=xts[xb][:], scalar1=rstd_all[:, wt:wt + 1],
                                                                                     scalar2=None, op0=ALU.mult),
                                 reads=[("xt", xb), ("rstdall", wt)], writes=[("xn", t)], cost=0.8)
                        yield
                    for c in range(8):
                        tp = bank(gr_[0], gr_[1])

                        def fTr(e, tp=tp, c=c):
                            ins = None
                            for t in range(4):
                                ins = e.matmul(tp[:, 128 * t:128 * t + 128], lhsT=xn4[:, t, 128 * c:128 * c + 128],
                                               rhs=ident, start=True, stop=True)
                            return ins
                        P.op("pe", fTr, reads=[("xn", t) for t in range(4)] + ["cbf"], writes=[pk(tp)], cost=0.5)
                        if (c % 2 == 0) if wb <= 4 else (c % 4 != 3):
                            P.op("dve", lambda e, tp=tp, c=c: e.tensor_scalar(
                                out=hT[:, c, :], in0=tp[:], scalar1=vcol(V_G1 + c), scalar2=None, op0=ALU.mult),
                                reads=[pk(tp), "vecs"], writes=[("hT", sq_ % len(hTs), c)], cost=0.75)
                        else:
                            P.op("act", lambda e, tp=tp, c=c: e.activation(
                                out=hT[:, c, :], in_=tp[:], func=ACT.Copy, scale=vcol(V_G1 + c)),
                                reads=[pk(tp), "vecs"], writes=[("hT", sq_ % len(hTs), c)])
                        yield
                    for pair in range(2):
                        pb = proj_fm(WK + 128 * pair, 512, 0)
                        yield
                        for _ in qk_post(pb, False, wb, pair):
                            yield
                    for t in range(4):
                        kt = 4 * wb + t
                        pb = bank(gr_[0], gr_[1])

                        def fV(e, pb=pb, t=t):
                            ins = None
                            for kc in range(8):
                                ins = e.matmul(pb[:, 0:256], lhsT=hT[:, kc, 128 * t:128 * t + 128], rhs=wsb[:, kc, WV:WV + 256],
                                               start=(kc == 0), stop=(kc == 7))
                            return ins
                        P.op("pe", fV, reads=wread(WV, 256) + [("hT", sq_ % len(hTs), c) for c in range(8)], writes=[pk(pb)], cost=1.0)
                        vdst = Vaug[:, kt, :].rearrange("p (a b) -> p a b", a=2)
                        vsrc = pb[:, 0:256].rearrange("p (a b) -> p a b", a=2)
                        if wb <= 4:
                            P.op("act", lambda e, vdst=vdst, vsrc=vsrc: e.copy(out=vdst[:, :, 0:64], in_=vsrc[:, :, 0:64]),
                                 reads=[pk(pb), ("Vaug_ones", kt // 8)], writes=[("Vaug", kt)], cost=0.45)
                            P.op("act", lambda e, vdst=vdst, vsrc=vsrc: e.copy(out=vdst[:, :, 128:192], in_=vsrc[:, :, 64:128]),
                                 reads=[pk(pb), ("Vaug_ones", kt // 8)], writes=[("Vaug", kt)], cost=0.45)
                        else:
                            P.op("dve", lambda e, vdst=vdst, vsrc=vsrc: e.tensor_copy(out=vdst[:, :, 0:64], in_=vsrc[:, :, 0:64]),
                                 reads=[pk(pb), ("Vaug_ones", kt // 8)], writes=[("Vaug", kt)], cost=0.35)
                            P.op("dve", lambda e, vdst=vdst, vsrc=vsrc: e.tensor_copy(out=vdst[:, :, 128:192], in_=vsrc[:, :, 64:128]),
                                 reads=[pk(pb), ("Vaug_ones", kt // 8)], writes=[("Vaug", kt)], cost=0.35)
                        yield
                    if own:
                        for pair in range(2):
                            pb = proj_fm(WQ + 128 * pair, 512, 0)
                            yield
                            for _ in qk_post(pb, True, wb, pair):
                                yield
                        for _ in range(4):
                            yield
                        if do_gating:
                            for _ in gating(wb - 4):
                                yield
                    if pas == 0 and (own or wb == 3):
                        for c in range(4):
                            if own:
                                ntok, tok0 = 512, 0
                            else:
                                ntok, tok0 = 128, 384
                            pa = proj_fm(WU + 128 * c, ntok, tok0)
                            pg = proj_fm(WU + 512 + 128 * c, ntok, tok0)
                            P.op("act", lambda e, pg=pg, c=c, ntok=ntok: e.activation(
                                out=sig[:, 0:ntok], in_=pg[:, 0:ntok], func=ACT.Tanh, bias=vcol(V_HBG + c), scale=0.5),
                                reads=[pk(pg), "vecs"], writes=["sig"], cost=0.7)
                            P.op("dve", lambda e, pa=pa, c=c, ntok=ntok: e.scalar_tensor_tensor(
                                out=sig[:, 0:ntok], in0=pa[:, 0:ntok], scalar=vcol(V_GLU + c), in1=sig[:, 0:ntok],
                                op0=ALU.add, op1=ALU.mult),
                                reads=[pk(pa), "sig", "vecs"], writes=["sig"], cost=0.75)
                            if own:
                                dst0 = 32 + (wb - 4) * 512
                                P.op("dve", lambda e, pa=pa, c=c, dst0=dst0: e.scalar_tensor_tensor(
                                    out=hglu[:, c, dst0:dst0 + 512], in0=pa[:, 0:512], scalar=vcol(V_GLU + c), in1=sig[:, 0:512],
                                    op0=ALU.add, op1=ALU.add),
                                    reads=[pk(pa), "sig", "vecs"], writes=[("hglu", c)], cost=0.75)
                            else:
                                P.op("dve", lambda e, pa=pa, c=c: e.scalar_tensor_tensor(
                                    out=hglu[:, c, 0:32], in0=pa[:, 96:128], scalar=vcol(V_GLU + c), in1=sig[:, 96:128],
                                    op0=ALU.add, op1=ALU.add),
                                    reads=[pk(pa), "sig", "vecs", "hglu_halo"], writes=[("hglu", c)], cost=0.3)
                                P.op("dve", lambda e, c=c: e.tensor_scalar(
                                    out=hglu[:, c, 0:32], in0=hglu[:, c, 0:32], scalar1=vcol(V_FLAG), scalar2=None, op0=ALU.mult),
                                    reads=[("hglu", c), "vecs"], writes=[("hglu", c)], cost=0.3)
                            yield

                def att_gen(qc):
                    for _ in attention(qc):
                        yield

                def run_gens(gens):
                    gens = list(gens)
                    while gens:
                        for g in list(gens):
                            try:
                                next(g)
                            except StopIteration:
                                gens.remove(g)

                run_gens([prep_gen(0)])
                for h in range(4):
                    P.op("sp", lambda e, h=h: e.dma_start(out=Kaug[h][64:80, :], in_=oneh_d),
                         reads=[("Kaug", 0, 0)], writes=[("Kaug_oh", h)], dma=True)
                if pas == 0:
                    issue_w(0, wsb, which=[0, 3], after=[("Kaug", 0, 0)])
                run_gens([prep_gen(1)])
                run_gens([prep_gen(4, do_gating=False)])
                for wb in range(2, 4):
                    run_gens([prep_gen(wb)])
                if pas == 1:
                    for half in range(2):
                        P.op("pool", lambda e, half=half: e.dma_start(
                            out=woutb[:, 4 * half:4 * half + 4, :],
                            in_=w_out[512 * half:512 * half + 512, :].rearrange("(c p) n -> p c n", p=128)),
                            reads=[("Kaug", 0, 3)], writes=[("woutb", half)], dma=True, xfer=15.0)
                run_gens([gating(0)])
                run_gens([att_gen(0), prep_gen(5)])
                run_gens([att_gen(1), prep_gen(6)])
                run_gens([att_gen(2), prep_gen(7)])
                run_gens([att_gen(3)])

                with nc.Block() as blk:
                    P.emit(blk)
                P.barrier()

            if pas == 0:
                for c_ in (2, 3, 4, 5, 6, 7):
                    mixc[c_] = SB(outer, "mixT%d" % c_, [128, T], BF16)
                wsb1 = SB(w1scope, "wsb1", [128, 8, 768], BF16)
                with ExitStack() as sc:
                    diagW = SB(sc, "diagW", [128, 4 * 31, 128], BF16)
                    y32 = [SB(sc, "y32_%d" % i, [128, 4, 512], F32) for i in range(2)]
                    ybf = [SB(sc, "ybf%d" % i, [128, 512], BF16) for i in range(2)]
                    ysq = [SB(sc, "ysq%d" % i, [128, 512], BF16) for i in range(2)]
                    mean = SB(sc, "mean", [128, 512], F32)
                    msq = SB(sc, "msq", [128, 512], F32)
                    var = SB(sc, "var", [128, 512], F32)
                    rstdc = SB(sc, "rstdc", [128, 512], F32)
                    zc = [SB(sc, "zc%d" % i, [128, 512], F32) for i in range(2)]
                    CR = [ps[0], ps[1], ps[2], ps[3]]
                    segs1 = issue_w(1, wsb1)
                    P.op("dve", lambda e: e.tensor_scalar(out=dww[:], in0=dww[:], scalar1=0.5, scalar2=None, op0=ALU.mult),
                         reads=["dww"], writes=["dww"], cost=0.3)
                    for c in range(4):
                        for i in range(31):
                            if i % 2 == 0:
                                P.op("dve", lambda e, c=c, i=i: e.tensor_scalar(
                                    out=diagW[:, c * 31 + i, :], in0=ident, scalar1=dww[:, c * 31 + i:c * 31 + i + 1],
                                    scalar2=None, op0=ALU.mult),
                                    reads=["cbf", "dww"], writes=[("diagW", c, i)], cost=0.3)
                            else:
                                P.op("act", lambda e, c=c, i=i: e.activation(
                                    out=diagW[:, c * 31 + i, :], in_=ident, func=ACT.Copy, scale=dww[:, c * 31 + i:c * 31 + i + 1]),
                                    reads=["cbf", "dww"], writes=[("diagW", c, i)], cost=0.4)
                    pending = []

                    def ln_block(tb, s1, s2):
                        P.op("act", lambda e, s1=s1: e.activation(out=mean[:], in_=s1[:], func=ACT.Copy, scale=1.0 / 512.0),
                             reads=[pk(s1)], writes=["mean"])
                        P.op("dve", lambda e: e.tensor_tensor(out=msq[:], in0=mean[:], in1=mean[:], op=ALU.mult),
                             reads=["mean"], writes=["msq"], cost=0.7)
                        P.op("dve", lambda e, s2=s2: e.scalar_tensor_tensor(out=var[:], in0=s2[:], scalar=1.0 / 512.0, in1=msq[:],
                                                                             op0=ALU.mult, op1=ALU.subtract),
                             reads=[pk(s2), "msq"], writes=["var"])
                        P.op("act", lambda e: e.activation(out=var[:], in_=var[:], func=ACT.Ln, scale=1.0, bias=vcol(V_E1)),
                             reads=["var", "vecs"], writes=["var"])
                        P.op("act", lambda e: e.activation(out=rstdc[:], in_=var[:], func=ACT.Exp, scale=-0.5),
                             reads=["var"], writes=["rstdc"])
                        for c in range(4):
                            z = zc[c % 2]
                            P.op("pool" if c % 2 == 0 else "dve", lambda e, c=c, z=z, tb=tb: e.tensor_tensor(out=z[:], in0=y32[tb % 2][:, c, :], in1=mean[:], op=ALU.subtract),
                                 reads=[("y32", tb % 2, c), "mean"], writes=[("zc", c % 2)], cost=(1.3 if c % 2 == 0 else 0.7))
                            P.op("dve", lambda e, c=c, z=z: e.tensor_tensor(out=z[:], in0=z[:], in1=rstdc[:], op=ALU.mult),
                                 reads=[("zc", c % 2), "rstdc"], writes=[("zc", c % 2)])
                            P.op("act", lambda e, c=c, z=z, tb=tb: e.activation(
                                out=mixc[4 + c][:, tb * 512:(tb + 1) * 512], in_=z[:], func=ACT.Silu,
                                scale=vcol(V_LNG + c), bias=vcol(V_LNB + c)),
                                reads=[("zc", c % 2), "vecs"], writes=[("mixT", 4 + c, tb)])

                    for tb in range(4):
                        s1 = bank("c2", [ps[4], ps[6]])
                        s2 = bank("c3", [ps[5], ps[7]])
                        for c in range(4):
                            pc = bank("c", CR)

                            def fC(e, pc=pc, c=c, tb=tb):
                                ins = None
                                for i in range(31):
                                    a0 = 32 + tb * 512 - 30 + i
                                    ins = e.matmul(pc[:], lhsT=diagW[:, c * 31 + i, :], rhs=hglu[:, c, a0:a0 + 512],
                                                   start=(i == 0), stop=(i == 30))
                                return ins
                            P.op("pe", fC, reads=[("diagW", c, i) for i in range(31)] + [("hglu", c), "hglu_halo"], writes=[pk(pc)], cost=6.8)
                            for fn in pending:
                                fn()
                            pending = []
                            P.op("dve", lambda e, pc=pc, c=c, tb=tb: e.tensor_scalar(
                                out=y32[tb % 2][:, c, :], in0=pc[:], scalar1=vcol(V_DWB + c), scalar2=None, op0=ALU.add),
                                reads=[pk(pc), "vecs"], writes=[("y32", tb % 2, c)])
                            P.op("act", lambda e, c=c, tb=tb: e.copy(out=ybf[c % 2][:], in_=y32[tb % 2][:, c, :]),
                                 reads=[("y32", tb % 2, c)], writes=[("ybf", c % 2)], cost=0.6)
                            P.op("dve", lambda e, c=c, tb=tb: e.tensor_tensor(out=ysq[c % 2][:], in0=y32[tb % 2][:, c, :], in1=y32[tb % 2][:, c, :], op=ALU.mult),
                                 reads=[("y32", tb % 2, c)], writes=[("ysq", c % 2)])

                            def stats(c=c, s1=s1, s2=s2, tb=tb):
                                P.op("pe", lambda e: e.matmul(s1[:], lhsT=allones, rhs=ybf[c % 2][:], start=(c == 0), stop=(c == 3)),
                                     reads=[("ybf", c % 2), "cbf"], writes=[pk(s1)])
                                P.op("pe", lambda e: e.matmul(s2[:], lhsT=allones, rhs=ysq[c % 2][:], start=(c == 0), stop=(c == 3)),
                                     reads=[("ysq", c % 2), "cbf"], writes=[pk(s2)])
                                if c == 3:
                                    ln_block(tb, s1, s2)
                            pending.append(stats)
                    for fn in pending:
                        fn()
                    with nc.Block() as blk:
                        P.emit(blk)
                    P.barrier()
                hscope.close()
                woutb = outer.enter_context(nc.sbuf_tensor("woutb", [128, 8, D], BF16, side="right"))

        w1scope.close()
        if DEBUG:
            for c in range(8):
                P.op("sp", lambda e, c=c: e.dma_start(out=dbg[:, c * T:(c + 1) * T], in_=mixc[c][:, :]), dma=True)
        with ExitStack() as sc:
            wdb = SB(sc, "wdb", [128, NFF, D], BF16)
            wgu = [SB(sc, "wgu%d" % i, [128, 8, 256], BF16) for i in range(4)]
            actT = SB(sc, "actT", [128, NFF, 512], BF16)
            x1 = [SB(sc, "x1_%d" % i, [128, 4, D], F32) for i in range(2)]
            xt2 = [SB(sc, "xt2_%d" % i, [128, D], F32) for i in range(2)]
            xn2 = [SB(sc, "xn2_%d" % i, [128, D], BF16) for i in range(2)]
            ssq2 = [SB(sc, "ssq2_%d" % i, [128, 1], F32) for i in range(2)]
            rstd2 = [SB(sc, "rstd2_%d" % i, [128, 1], F32) for i in range(2)]
            sg = [SB(sc, "sg%d" % i, [128, 512], F32) for i in range(2)]
            osb = [SB(sc, "osb%d" % i, [128, 512], F32) for i in range(2)]
            A8 = list(ps)


            def issue_wd():
                for q in range(2):
                    P.op("pool", lambda e, q=q: e.dma_start(
                        out=wdb[:, 11 * q:11 * q + 11, :],
                        in_=w_down[1408 * q:1408 * q + 1408, :].rearrange("(c p) n -> p c n", p=128)),
                        reads=[("wgu", 3)], writes=[("wdb", q)], dma=True, xfer=40.0, own=q)

            fcount = [0]

            def outnorm_gen(tb):
                xb1 = x1[tb % 2]
                for t in range(4):
                    tt = 4 * tb + t
                    xb = tt % 2
                    P.op("sp", lambda e, tt=tt, xb=xb: e.dma_start(out=xt2[xb][:], in_=xw[T + tt * 128:T + (tt + 1) * 128, :]),
                         writes=[("xt2", xb)], dma=True)
                    for half in range(2):
                        po = bank("a", A8)

                        def fOP(e, po=po, tt=tt, half=half):
                            ins = None
                            for c in range(8):
                                ins = e.matmul(po[:], lhsT=mixc[c][:, tt * 128:(tt + 1) * 128],
                                               rhs=woutb[:, c, 512 * half:512 * half + 512], start=(c == 0), stop=(c == 7))
                            return ins
                        P.op("pe", fOP, reads=[("mixT_tile", tt), ("woutb", 0), ("woutb", 1)], writes=[pk(po)], cost=1.8)
                        yield
                        P.op("dve", lambda e, po=po, xb1=xb1, t=t, half=half, xb=xb: e.tensor_tensor(
                            out=xb1[:, t, 512 * half:512 * half + 512], in0=po[:], in1=xt2[xb][:, 512 * half:512 * half + 512], op=ALU.add),
                            reads=[pk(po), ("xt2", xb)], writes=[("x1", tb % 2, t)])
                    yield
                    P.op("act", lambda e, xb=xb, xb1=xb1, t=t: e.activation(out=xn2[xb][:], in_=xb1[:, t, :], func=ACT.Square, accum_out=ssq2[xb][:]),
                         reads=[("x1", tb % 2, t)], writes=[("xn2", xb), ("ssq2", xb)], cost=1.05)
                    P.op("dve", lambda e, xb=xb: e.tensor_scalar(out=rstd2[xb][:], in0=ssq2[xb][:], scalar1=1.0 / D, scalar2=EPS, op0=ALU.mult, op1=ALU.add),
                         reads=[("ssq2", xb)], writes=[("rstd2", xb)], cost=0.2)
                    P.op("pool", lambda e, xb=xb: e.tensor_tensor(out=rstd2[xb][:], in0=rstd2[xb][:], in1=cneg[:, 1:2], op=ALU.pow),
                         reads=[("rstd2", xb), "cneg1"], writes=[("rstd2", xb)], cost=0.5)
                    yield
                    yield
                    P.op("dve", lambda e, xb=xb, xb1=xb1, t=t: e.tensor_scalar(out=xn2[xb][:], in0=xb1[:, t, :], scalar1=rstd2[xb][:],
                                                                                scalar2=None, op0=ALU.mult),
                         reads=[("x1", tb % 2, t), ("rstd2", xb)], writes=[("xn2", xb)])
                    yield
                    yield
                    for half in range(2):
                        tp = bank("a", A8)

                        def fTr(e, tp=tp, half=half, xb=xb):
                            ins = None
                            for cc in range(4):
                                c = 4 * half + cc
                                ins = e.matmul(tp[:, 128 * cc:128 * cc + 128], lhsT=xn2[xb][:, 128 * c:128 * c + 128],
                                               rhs=ident, start=True, stop=True)
                            return ins
                        P.op("pe", fTr, reads=[("xn2", xb), "cbf"], writes=[pk(tp)], cost=0.5)
                        yield
                        for cc in range(4):
                            c = 4 * half + cc
                            if cc % 2 == 0:
                                P.op("dve", lambda e, tp=tp, cc=cc, c=c, tt=tt: e.tensor_scalar(
                                    out=mixc[c][:, tt * 128:(tt + 1) * 128], in0=tp[:, 128 * cc:128 * cc + 128],
                                    scalar1=vcol(V_G2 + c), scalar2=None, op0=ALU.mult),
                                    reads=[pk(tp), "vecs"], writes=[("mixT_tile", tt)])
                            else:
                                P.op("act", lambda e, tp=tp, cc=cc, c=c, tt=tt: e.activation(
                                    out=mixc[c][:, tt * 128:(tt + 1) * 128], in_=tp[:, 128 * cc:128 * cc + 128],
                                    func=ACT.Copy, scale=vcol(V_G2 + c)),
                                    reads=[pk(tp), "vecs"], writes=[("mixT_tile", tt)])
                    yield

            def gateup_gen(tb):
                for f_ in range(NFF):
                    slot = fcount[0] % 4
                    fcount[0] += 1
                    wt_ = wgu[slot]
                    P.op("pool", lambda e, f_=f_, wt_=wt_: e.dma_start(
                        out=wt_[:, :, :], in_=w_gu[:, f_ * 256:(f_ + 1) * 256].rearrange("(c p) n -> p c n", p=128)),
                        writes=[("wgu", slot)], dma=True, xfer=6.0)
                    if tb == 0 and f_ == 3:
                        issue_wd()
                    pg = bank("a", A8)
                    pu = bank("a", A8)

                    def fGU(e, pg=pg, pu=pu, wt_=wt_, tb=tb):
                        ins = None
                        for c in range(8):
                            ins = e.matmul(pg[:], lhsT=wt_[:, c, 0:128], rhs=mixc[c][:, tb * 512:(tb + 1) * 512],
                                           start=(c == 0), stop=(c == 7))
                        for c in range(8):
                            ins = e.matmul(pu[:], lhsT=wt_[:, c, 128:256], rhs=mixc[c][:, tb * 512:(tb + 1) * 512],
                                           start=(c == 0), stop=(c == 7))
                        return ins
                    tiles = [("mixT_tile", 4 * tb + t) for t in range(4)]
                    P.op("pe", fGU, reads=[("wgu", slot)] + tiles, writes=[pk(pg), pk(pu)], cost=3.6)
                    sgi = sg[f_ % 2]
                    P.op("act", lambda e, pg=pg, sgi=sgi: e.activation(out=sgi[:], in_=pg[:], func=ACT.Silu),
                         reads=[pk(pg)], writes=[("sg", f_ % 2)])
                    P.op("dve", lambda e, pu=pu, sgi=sgi, f_=f_: e.tensor_tensor(out=actT[:, f_, :], in0=pu[:], in1=sgi[:], op=ALU.mult),
                         reads=[pk(pu), ("sg", f_ % 2)], writes=[("actT", f_)])
                    yield

            def down_gen(tb):
                xb1 = x1[tb % 2]
                for t in range(4):
                    tt = 4 * tb + t
                    for half in range(2):
                        pd = bank("a", A8)

                        def fD(e, pd=pd, t=t, half=half):
                            ins = None
                            for f_ in range(NFF):
                                ins = e.matmul(pd[:], lhsT=actT[:, f_, 128 * t:128 * t + 128],
                                               rhs=wdb[:, f_, 512 * half:512 * half + 512], start=(f_ == 0), stop=(f_ == NFF - 1))
                            return ins
                        P.op("pe", fD, reads=[("actT", f_) for f_ in range(NFF)] + [("wdb", 0), ("wdb", 1)], writes=[pk(pd)], cost=4.9)
                        ob = osb[(2 * t + half) % 2]
                        P.op("dve", lambda e, pd=pd, ob=ob, xb1=xb1, t=t, half=half: e.tensor_tensor(
                            out=ob[:], in0=pd[:], in1=xb1[:, t, 512 * half:512 * half + 512], op=ALU.add),
                            reads=[pk(pd), ("x1", tb % 2, t)], writes=[("osb", (2 * t + half) % 2)])
                        P.op("sp", lambda e, ob=ob, tt=tt, half=half: e.dma_start(
                            out=y[tt * 128:(tt + 1) * 128, 512 * half:512 * half + 512], in_=ob[:]),
                            reads=[("osb", (2 * t + half) % 2)], dma=True)
                        yield

            def run_gens2(gens):
                gens = list(gens)
                while gens:
                    for g in list(gens):
                        try:
                            next(g)
                        except StopIteration:
                            gens.remove(g)

            run_gens2([outnorm_gen(0)])
            for tb in range(4):
                gl = [gateup_gen(tb)]
                if tb + 1 < 4:
                    gl.append(outnorm_gen(tb + 1))
                run_gens2(gl)
                run_gens2([down_gen(tb)])
            with nc.Block() as blk:
                P.emit(blk, final_wait=True)
    return nc


_NC_CACHE = {}


def _consts():
    bf = ml_dtypes.bfloat16
    p = np.arange(128)
    ident = np.eye(128, dtype=np.float32)
    tri = np.where(p[:, None] <= p[None, :], 0.0, NEG).astype(np.float32)
    blockones = (p[:, None] // 64 == p[None, :] // 64).astype(np.float32)
    partner = (p // 64) * 64 + (p % 64 + 32) % 64
    perm = (p[:, None] == partner[None, :]).astype(np.float32)
    allones = np.ones((128, 128), np.float32)
    cbf = np.concatenate([ident, tri, blockones, perm, allones], axis=1).astype(bf)
    keyblk = np.arange(W) // 256
    oneh = np.where(keyblk[None, :] == np.arange(16)[:, None], NEG, 0.0).astype(np.float32).astype(bf)
    return cbf, oneh


def _rope_tables(half):
    pos = (np.arange(W, dtype=np.int64) + half * T - T)
    posf = np.maximum(pos, 0).astype(np.float32)
    inv_freq = (np.float32(10000.0) ** (-np.arange(0, 64, 2, dtype=np.float32) / np.float32(64))).astype(np.float32)
    ang = (posf[:, None] * inv_freq[None, :]).astype(np.float32)
    ang = np.concatenate([ang, ang], axis=-1)
    cos = np.cos(ang).astype(np.float32).T
    sin = np.sin(ang).astype(np.float32).T
    sgn = np.where(np.arange(64) < 32, -1.0, 1.0).astype(np.float32)[:, None]
    sin = sin * sgn
    cos2 = np.concatenate([cos, cos], axis=0)
    sin2 = np.concatenate([sin, sin], axis=0)
    return np.ascontiguousarray(cos2), np.ascontiguousarray(sin2)


def _elig(half):
    el = np.zeros((16, 16), np.float32)
    for qt in range(16):
        ob = qt // 2
        for j in range(16):
            if j < 8:
                v = 0.0 if half == 1 else -BIGF
            elif j < 8 + ob:
                v = 0.0
            elif j == 8 + ob:
                v = BIGF
            else:
                v = -BIGF
            el[qt, j] = v
    inel = (el < -1e29).astype(np.float32)
    el_b = np.broadcast_to(el.reshape(1, 256), (128, 256)).copy()
    in_b = np.broadcast_to(inel.reshape(1, 256), (128, 256)).copy()
    return el_b, in_b


def kernel(x, norm1_g, w_in, glu_b, q_norm_g, k_norm_g, dw_w, dw_b, conv_ln_g, conv_ln_b,
           w_out, norm2_g, w_gate, w_up, w_down):
    f32 = np.float32
    x = np.asarray(x, f32)

    def fm(v, n):
        return np.asarray(v, f32).reshape(n, 128).T

    vecs = np.zeros((128, 64), f32)
    vecs[:, V_G1:V_G1 + 8] = fm(norm1_g[0], 8)
    vecs[:, V_G2:V_G2 + 8] = fm(norm2_g[0], 8)
    vecs[:, V_GLU:V_GLU + 8] = fm(glu_b[0], 8)
    vecs[:, V_GQ] = np.tile(np.asarray(q_norm_g[0], f32), 2)
    vecs[:, V_GK] = np.tile(np.asarray(k_norm_g[0], f32), 2)
    vecs[:, V_DWB:V_DWB + 4] = fm(dw_b[0], 4)
    vecs[:, V_LNG:V_LNG + 4] = fm(conv_ln_g[0], 4)
    vecs[:, V_LNB:V_LNB + 4] = fm(conv_ln_b[0], 4)
    dww = np.asarray(dw_w[0], f32).reshape(31, 4, 128).transpose(2, 1, 0).reshape(128, 4 * 31)
    dww = np.ascontiguousarray(dww)
    cbf, oneh = _consts()

    shared = {
        "w_in": np.ascontiguousarray(np.asarray(w_in[0], f32)),
        "w_out": np.ascontiguousarray(np.asarray(w_out[0], f32)),
        "w_gu": np.ascontiguousarray(np.concatenate(
            [np.asarray(w_gate[0], f32).reshape(D, NFF, 128), np.asarray(w_up[0], f32).reshape(D, NFF, 128)], axis=2).reshape(D, NFF * 256)),
        "w_down": np.ascontiguousarray(np.asarray(w_down[0], f32)),
        "dww": dww, "cbf": cbf, "oneh": oneh,
    }
    tabs = [_rope_tables(h) for h in range(2)]
    eligs = [_elig(h) for h in range(2)]
    in_maps = []
    for core in range(8):
        b, half = core // 2, core % 2
        xw = np.zeros((W, D), f32)
        if half == 1:
            xw[:] = x[b]
        else:
            xw[T:] = x[b, :T]
        v = vecs.copy()
        v[:, V_FLAG] = float(half)
        v[:, V_E1] = EPS
        v[:, V_E64] = 64.0 * EPS
        m = dict(shared)
        m.update({"xw": xw, "vecs": v, "cos_t": tabs[half][0], "sin_t": tabs[half][1],
                  "elig": eligs[half][0], "inel": eligs[half][1]})
        in_maps.append(m)

    if "nc" not in _NC_CACHE:
        _NC_CACHE["nc"] = build_program()
    nc = _NC_CACHE["nc"]
    res = run_bass_kernel_spmd(nc, in_maps, core_ids=list(range(8)))
    out = np.zeros((B, S, D), f32)
    for core in range(8):
        b, half = core // 2, core % 2
        out[b, half * T:(half + 1) * T] = res.results[core]["y"]
    if DEBUG:
        kernel.dbg = [res.results[c]["dbg"] for c in range(8)]
    return out
```
